# Optimizing a Trainium2 kernel written in Bass

```python
import math
import jax, jax.numpy as jnp
from jax import lax
import numpy as np

D_MODEL = 1024
BATCH = 8
SEQ = 2048
DEPTH = 2
DEC_BATCH = 128
DEC_SEQ = 8
PAST_LEN = 16384
PAGE_SIZE = 128

D_MIX = D_MODEL
DN_HEAD_DIM = 128
DN_WIDTH = D_MIX // 2
DN_HEADS = DN_WIDTH // DN_HEAD_DIM
DN_CONV = 4
DN_CHUNK = 64
POOL_WINDOWS = (2, 4, 8, 16)
POOL_GROUPS = len(POOL_WINDOWS)
POOL_WIDTH = D_MIX // 4
POOL_GROUP_DIM = POOL_WIDTH // POOL_GROUPS
POOL_BUF = max(POOL_WINDOWS) - 1
SC_WIDTH = D_MIX - DN_WIDTH - POOL_WIDTH
SC_CONV = 3
D_FF = 2816
FFN_CONV = 3
EPS = 1e-6
SPLITS = (3 * DN_WIDTH, DN_WIDTH, DN_HEADS, DN_HEADS, POOL_WIDTH, SC_WIDTH, SC_WIDTH, SC_WIDTH)
D_IN = sum(SPLITS)

kernel_name = "hybrid_delta_pool_shortconv_decoder_step"


def rmsnorm(x, g):
    xf = x.astype(jnp.float32)
    y = xf * lax.rsqrt(jnp.mean(xf * xf, axis=-1, keepdims=True) + EPS)
    return (y * g.astype(jnp.float32)).astype(x.dtype)


def l2norm(t):
    tf = t.astype(jnp.float32)
    return tf * lax.rsqrt(jnp.sum(tf * tf, axis=-1, keepdims=True) + EPS)


def causal_dwconv(x, w, buf):
    width = w.shape[0]
    L = x.shape[1]
    xp = jnp.concatenate([buf.astype(x.dtype), x], axis=1)
    y = xp[:, 0:L] * w[0]
    for i in range(1, width):
        y = y + xp[:, i:i + L] * w[i]
    return y, xp[:, xp.shape[1] - (width - 1):]


def gated_delta_rule(q, k, v, g, beta, s0):
    f32 = jnp.float32
    bsz, L, h, dk = q.shape
    dv = v.shape[-1]
    c = min(DN_CHUNK, L)
    n = -(-L // c)
    pad = n * c - L

    def prep(t):
        t = t.astype(f32)
        t = jnp.pad(t, [(0, 0), (0, pad)] + [(0, 0)] * (t.ndim - 2))
        t = t.reshape((bsz, n, c) + t.shape[2:])
        return jnp.moveaxis(t, 3, 1)

    q, k, v, g, beta = prep(q), prep(k), prep(v), prep(g), prep(beta)
    q = q * (dk ** -0.5)
    gc = jnp.cumsum(g, axis=-1)
    idx = jnp.arange(c)
    causal = idx[:, None] >= idx[None, :]
    strict = idx[:, None] > idx[None, :]
    decay = jnp.exp(jnp.where(causal, gc[..., :, None] - gc[..., None, :], -jnp.inf))
    kb = k * beta[..., None]
    a = jnp.einsum('bhnid,bhnjd->bhnij', kb, k) * jnp.where(strict, decay, 0.0)
    eye = jnp.eye(c, dtype=f32)
    rhs = jnp.concatenate([v * beta[..., None], kb * jnp.exp(gc)[..., None]], axis=-1)
    sol = lax.linalg.triangular_solve(eye + a, rhs, left_side=True, lower=True)
    u, w = sol[..., :dv], sol[..., dv:]
    qk = jnp.einsum('bhnid,bhnjd->bhnij', q, k) * decay
    qg = q * jnp.exp(gc)[..., None]
    g_last = gc[..., -1]
    kd = k * jnp.exp(g_last[..., None] - gc)[..., None]

    def step(s, xs):
        u_i, w_i, qk_i, qg_i, kd_i, gl_i = xs
        v_new = u_i - jnp.einsum('bhck,bhkv->bhcv', w_i, s)
        o = jnp.einsum('bhck,bhkv->bhcv', qg_i, s) + jnp.einsum('bhij,bhjv->bhiv', qk_i, v_new)
        s = s * jnp.exp(gl_i)[..., None, None] + jnp.einsum('bhck,bhcv->bhkv', kd_i, v_new)
        return s, o

    xs = tuple(jnp.moveaxis(t, 2, 0) for t in (u, w, qk, qg, kd, g_last))
    s, o = lax.scan(step, s0.astype(f32), xs)
    o = jnp.moveaxis(o, 0, 2)
    o = jnp.moveaxis(o, 1, 3).reshape(bsz, n * c, h, dv)[:, :L]
    return o, s


def delta_mixer(qkv, z, a, bg, conv_w, a_log, dt_bias, out_norm, s0, buf):
    bsz, L, _ = qkv.shape
    cv, new_buf = causal_dwconv(qkv, conv_w, buf)
    cv = jax.nn.silu(cv)
    q, k, v = jnp.split(cv, 3, axis=-1)
    q = l2norm(q.reshape(bsz, L, DN_HEADS, DN_HEAD_DIM))
    k = l2norm(k.reshape(bsz, L, DN_HEADS, DN_HEAD_DIM))
    v = v.reshape(bsz, L, DN_HEADS, DN_HEAD_DIM)
    g = -jnp.exp(a_log.astype(jnp.float32)) * jax.nn.softplus(a.astype(jnp.float32) + dt_bias.astype(jnp.float32))
    beta = jax.nn.sigmoid(bg.astype(jnp.float32))
    o, s = gated_delta_rule(q, k, v, g, beta, s0)
    zf = z.reshape(bsz, L, DN_HEADS, DN_HEAD_DIM).astype(jnp.float32)
    o = o * lax.rsqrt(jnp.mean(o * o, axis=-1, keepdims=True) + EPS) * out_norm.astype(jnp.float32) * jax.nn.silu(zf)
    return o.reshape(bsz, L, DN_WIDTH).astype(qkv.dtype), s, new_buf


def pool_mixer(xin, buf, w_grp, scale, pos0):
    bsz, L, wd = xin.shape
    xf = xin.astype(jnp.float32)
    ext = jnp.concatenate([buf.astype(jnp.float32), xf], axis=1)
    cs = jnp.concatenate([jnp.zeros((bsz, 1, wd), jnp.float32), jnp.cumsum(ext, axis=1)], axis=1)
    pos = jnp.arange(L) + pos0
    hi = cs[:, POOL_BUF + 1:POOL_BUF + 1 + L]
    outs = []
    for gi, win in enumerate(POOL_WINDOWS):
        lo_c, hi_c = gi * POOL_GROUP_DIM, (gi + 1) * POOL_GROUP_DIM
        ssum = hi[..., lo_c:hi_c] - cs[:, POOL_BUF + 1 - win:POOL_BUF + 1 - win + L, lo_c:hi_c]
        cnt = jnp.minimum(pos + 1, win).astype(jnp.float32)
        outs.append(ssum / cnt[None, :, None])
    d = (jnp.concatenate(outs, axis=-1) - xf).reshape(bsz, L, POOL_GROUPS, POOL_GROUP_DIM)
    y = jnp.einsum('blgc,gcd->blgd', d, w_grp.astype(jnp.float32)).reshape(bsz, L, wd) * scale.astype(jnp.float32)
    return y.astype(xin.dtype), ext[:, ext.shape[1] - POOL_BUF:].astype(xin.dtype)


def trunk(x, states, pos0, params):
    st_delta, st_dconv, st_pool, st_sconv, st_fconv = states
    (norm_mix, w_in, dn_conv_w, dn_a_log, dn_dt_bias, dn_out_norm, pool_w, pool_scale,
     sconv_w, w_out, norm_ffn, w_ffn_gate, ffn_conv_w, w_ffn_up, w_ffn_down, final_norm) = params
    sd, sdc, sp, ssc, sfc = [], [], [], [], []
    cuts = list(np.cumsum(SPLITS)[:-1])
    for l in range(DEPTH):
        h = rmsnorm(x, norm_mix[l])
        proj = h @ w_in[l]
        qkv, z, a, bg, p_in, sc_x, sc_b, sc_c = jnp.split(proj, cuts, axis=-1)
        o_a, s_new, dbuf = delta_mixer(qkv, z, a, bg, dn_conv_w[l], dn_a_log[l], dn_dt_bias[l],
                                       dn_out_norm[l], st_delta[l], st_dconv[l])
        o_b, pbuf = pool_mixer(p_in, st_pool[l], pool_w[l], pool_scale[l], pos0)
        cconv, cbuf = causal_dwconv(sc_c * sc_x, sconv_w[l], st_sconv[l])
        o_c = sc_b * cconv
        x = x + jnp.concatenate([o_a, o_b, o_c], axis=-1) @ w_out[l]
        h = rmsnorm(x, norm_ffn[l])
        gate, fbuf = causal_dwconv(h @ w_ffn_gate[l], ffn_conv_w[l], st_fconv[l])
        x = x + (jax.nn.silu(gate) * (h @ w_ffn_up[l])) @ w_ffn_down[l]
        sd.append(s_new); sdc.append(dbuf); sp.append(pbuf); ssc.append(cbuf); sfc.append(fbuf)
    y = rmsnorm(x, final_norm)
    return y, (jnp.stack(sd), jnp.stack(sdc), jnp.stack(sp), jnp.stack(ssc), jnp.stack(sfc))


def setup_inputs(seed: int = 0) -> dict:
    key = jax.random.key(seed)
    ks = iter(jax.random.split(key, 32))

    def nrm(shape, scale=1.0):
        return jax.random.normal(next(ks), shape, jnp.float32) * scale

    dt = jnp.exp(jax.random.uniform(next(ks), (DEPTH, DN_HEADS), jnp.float32,
                                    minval=math.log(1e-3), maxval=math.log(1e-1)))
    return {
        "x_prompt": nrm((BATCH, SEQ, D_MODEL)),
        "x_sample": nrm((DEC_BATCH, DEC_SEQ, D_MODEL)),
        "state_delta": nrm((DEPTH, DEC_BATCH, DN_HEADS, DN_HEAD_DIM, DN_HEAD_DIM), 0.1),
        "state_delta_conv": nrm((DEPTH, DEC_BATCH, DN_CONV - 1, 3 * DN_WIDTH)),
        "state_pool": nrm((DEPTH, DEC_BATCH, POOL_BUF, POOL_WIDTH)),
        "state_sconv": nrm((DEPTH, DEC_BATCH, SC_CONV - 1, SC_WIDTH)),
        "state_ffn_conv": nrm((DEPTH, DEC_BATCH, FFN_CONV - 1, D_FF)),
        "norm_mix": 1.0 + nrm((DEPTH, D_MODEL), 0.05),
        "w_in": nrm((DEPTH, D_MODEL, D_IN), D_MODEL ** -0.5),
        "dn_conv_w": nrm((DEPTH, DN_CONV, 3 * DN_WIDTH), 0.5),
        "dn_a_log": jnp.log(jax.random.uniform(next(ks), (DEPTH, DN_HEADS), jnp.float32, minval=1.0, maxval=16.0)),
        "dn_dt_bias": dt + jnp.log(-jnp.expm1(-dt)),
        "dn_out_norm": 1.0 + nrm((DEPTH, DN_HEAD_DIM), 0.05),
        "pool_w": nrm((DEPTH, POOL_GROUPS, POOL_GROUP_DIM, POOL_GROUP_DIM), POOL_GROUP_DIM ** -0.5),
        "pool_scale": 0.5 + nrm((DEPTH, POOL_WIDTH), 0.1),
        "sconv_w": nrm((DEPTH, SC_CONV, SC_WIDTH), 0.5),
        "w_out": nrm((DEPTH, D_MIX, D_MODEL), D_MIX ** -0.5),
        "norm_ffn": 1.0 + nrm((DEPTH, D_MODEL), 0.05),
        "w_ffn_gate": nrm((DEPTH, D_MODEL, D_FF), D_MODEL ** -0.5),
        "ffn_conv_w": nrm((DEPTH, FFN_CONV, D_FF), 0.5),
        "w_ffn_up": nrm((DEPTH, D_MODEL, D_FF), D_MODEL ** -0.5),
        "w_ffn_down": nrm((DEPTH, D_FF, D_MODEL), D_FF ** -0.5),
        "final_norm": 1.0 + nrm((D_MODEL,), 0.05),
    }


def reference(x_prompt, x_sample, state_delta, state_delta_conv, state_pool, state_sconv, state_ffn_conv,
              norm_mix, w_in, dn_conv_w, dn_a_log, dn_dt_bias, dn_out_norm, pool_w, pool_scale,
              sconv_w, w_out, norm_ffn, w_ffn_gate, ffn_conv_w, w_ffn_up, w_ffn_down, final_norm):
    params = (norm_mix, w_in, dn_conv_w, dn_a_log, dn_dt_bias, dn_out_norm, pool_w, pool_scale,
              sconv_w, w_out, norm_ffn, w_ffn_gate, ffn_conv_w, w_ffn_up, w_ffn_down, final_norm)
    dt = x_prompt.dtype
    zero_states = (
        jnp.zeros((DEPTH, BATCH, DN_HEADS, DN_HEAD_DIM, DN_HEAD_DIM), jnp.float32),
        jnp.zeros((DEPTH, BATCH, DN_CONV - 1, 3 * DN_WIDTH), dt),
        jnp.zeros((DEPTH, BATCH, POOL_BUF, POOL_WIDTH), dt),
        jnp.zeros((DEPTH, BATCH, SC_CONV - 1, SC_WIDTH), dt),
        jnp.zeros((DEPTH, BATCH, FFN_CONV - 1, D_FF), dt),
    )
    y_prompt, p_st = trunk(x_prompt, zero_states, 0, params)
    y_sample, s_st = trunk(x_sample, (state_delta, state_delta_conv, state_pool, state_sconv, state_ffn_conv),
                           PAST_LEN, params)
    return (y_prompt, y_sample,
            p_st[0].astype(state_delta.dtype), p_st[1].astype(state_delta_conv.dtype),
            p_st[2].astype(state_pool.dtype), p_st[3].astype(state_sconv.dtype),
            p_st[4].astype(state_ffn_conv.dtype),
            s_st[0].astype(state_delta.dtype), s_st[1].astype(state_delta_conv.dtype),
            s_st[2].astype(state_pool.dtype), s_st[3].astype(state_sconv.dtype),
            s_st[4].astype(state_ffn_conv.dtype))
```

```python
import numpy as np
from contextlib import ExitStack
import concourse.bass as bass
import concourse.mybir as mybir
from concourse.bass_utils import run_bass_kernel_spmd

F32 = mybir.dt.float32
BF16 = mybir.dt.bfloat16
AF = mybir.ActivationFunctionType
ALU = mybir.AluOpType
AX = mybir.AxisListType

GRAN = 128
_DSZ = {F32: 4, BF16: 2}
EPS = 1e-6
NCORES = 8
D = 1024
DFF = 2816
NTOK = 2176
NPROMPT = 2048
BIG = 30000.0


class T:
    def __init__(self, ap, space, addr, nbytes, shape, dtype):
        self.ap, self.space, self.addr, self.nbytes = ap, space, addr, nbytes
        self.shape, self.dtype = list(shape), dtype

    def __getitem__(self, k):
        return self.ap[k]

    ivl = None
    subl = None

    def iv(self):
        return (self.space, self.addr, self.addr + self.nbytes)

    def sub(self, i, n=1):
        if self.subl is not None:
            return self.subl[i]
        per = self.nbytes // self.shape[1]
        return (self.space, self.addr + i * per, self.addr + (i + n) * per)


class Sched:
    ENGS = ("pe", "act", "dve", "pool", "sp")

    def __init__(self, nc, sb_base, sb_limit):
        self.nc = nc
        self.ops = {e: [] for e in self.ENGS}
        self.cnt = {e: 0 for e in self.ENGS}
        self.known = {e: {} for e in self.ENGS}
        self.gw = {}
        self.gr = {}
        self.dma_cnt = {}
        self.sb_ptr = sb_base
        self.sb_limit = sb_limit
        self.n_id = 0
        self.total = 0
        self.limit = 1 << 60
        self.marks = []

    def sb(self, name, shape, dtype, addr=None):
        n = 1
        for s in shape[1:]:
            n *= s
        nbytes = n * _DSZ[dtype]
        nb_al = (nbytes + GRAN - 1) // GRAN * GRAN
        if addr is None:
            addr = self.sb_ptr
            self.sb_ptr += nb_al
        assert addr + nb_al <= self.sb_limit, f"SBUF overflow at {name}: {addr + nb_al}"
        self.n_id += 1
        h = self.nc.alloc_sbuf_tensor_at(f"{name}_{self.n_id}", list(shape), dtype, offset=addr)
        return T(h[:], "sb", addr, nbytes, shape, dtype)

    @staticmethod
    def _ivs(items):
        out = []
        flat = []
        for it in items:
            if isinstance(it, T) and it.ivl is not None:
                flat.extend(it.ivl)
            else:
                flat.append(it)
        for it in flat:
            iv = it.iv() if isinstance(it, T) else it
            if iv[0] == "ps":
                b = iv[1] // 2048
                iv = ("ps", b * 2048, b * 2048 + 2048)
            out.append(iv)
        return out

    @staticmethod
    def _psfix(reads, writes):
        r2 = [iv for iv in reads if iv[0] != "ps"]
        w2 = list(writes) + [iv for iv in reads if iv[0] == "ps"]
        return r2, w2

    @staticmethod
    def _grans(iv):
        sp, lo, hi = iv
        return [(sp, g) for g in range(lo // GRAN, (hi + GRAN - 1) // GRAN)]

    def _collect(self, reads, writes):
        deps = {}
        gw, gr = self.gw, self.gr
        for iv in reads:
            for g in self._grans(iv):
                t = gw.get(g)
                if t is not None and deps.get(t[0], 0) < t[1]:
                    deps[t[0]] = t[1]
        for iv in writes:
            for g in self._grans(iv):
                t = gw.get(g)
                if t is not None and deps.get(t[0], 0) < t[1]:
                    deps[t[0]] = t[1]
                r = gr.get(g)
                if r:
                    for k, v in r.items():
                        if deps.get(k, 0) < v:
                            deps[k] = v
        return deps

    def _commit(self, reads, writes, tok):
        k, v = tok
        for iv in reads:
            for g in self._grans(iv):
                r = self.gr.setdefault(g, {})
                if r.get(k, 0) < v:
                    r[k] = v
        for iv in writes:
            for g in self._grans(iv):
                self.gw[g] = tok
                self.gr[g] = {}

    def _waits(self, eng, deps):
        kn = self.known[eng]
        w = []
        for k, v in deps.items():
            if kn.get(k, 0) < v:
                kn[k] = v
                w.append((k, v))
        return w

    def mark(self, name):
        self.marks.append((name, self.total))

    def op(self, eng, fn, reads=(), writes=()):
        self.total += 1
        if self.total > self.limit:
            return
        reads, writes = self._psfix(self._ivs(reads), self._ivs(writes))
        waits = self._waits(eng, self._collect(reads, writes))
        self.cnt[eng] += 1
        tok = (("e", eng), self.cnt[eng])
        self.known[eng][tok[0]] = 0 if False else self.known[eng].get(tok[0], 0)
        self.ops[eng].append((fn, waits, tok))
        self._commit(reads, writes, tok)

    def dma(self, eng, key, fn, reads=(), writes=()):
        self.total += 1
        if self.total > self.limit:
            return
        reads, writes = self._ivs(reads), self._ivs(writes)
        waits = self._waits(eng, self._collect(reads, writes))
        self.dma_cnt[key] = self.dma_cnt.get(key, 0) + 16
        tok = (("d", key), self.dma_cnt[key])
        self.ops[eng].append((fn, waits, tok))
        self._commit(reads, writes, tok)

    def emit(self):
        nc = self.nc
        fin = [(("d", k), v) for k, v in self.dma_cnt.items()]
        with ExitStack() as es:
            sems = {}
            for e in self.ENGS:
                sems[("e", e)] = es.enter_context(nc.semaphore(f"s_{e}"))
            for k in self.dma_cnt:
                sems[("d", k)] = es.enter_context(nc.semaphore(f"d_{k}"))
            block = es.enter_context(nc.Block())

            def run(eng_name):
                def body(eng):
                    for fn, waits, tok in self.ops[eng_name]:
                        for k, v in waits:
                            eng.wait_ge(sems[k], v)
                        ins = fn(eng)
                        ins.then_inc(sems[tok[0]], 16 if tok[0][0] == "d" else 1)
                    if eng_name == "sp":
                        for k, v in fin:
                            eng.wait_ge(sems[k], v)
                return body

            block.tensor(run("pe"))
            block.scalar(run("act"))
            block.vector(run("dve"))
            block.gpsimd(run("pool"))
            block.sync(run("sp"))


NT = 256
PTILES = [(i * NT, NT, 1, NT) for i in range(NPROMPT // NT)]
STILE = (NPROMPT, 128, 16, 8)
GROUPS = [[STILE] + PTILES[0:2], PTILES[2:4], PTILES[4:6], PTILES[6:8]]
MAXT = 3


class Builder:
    def __init__(self, nc):
        self.nc = nc
        self.s = Sched(nc, 16640, 229312)
        self.dr = {}

    def din(self, name, shape, dt=F32):
        self.dr[name] = self.nc.dram_tensor(name, list(shape), dt, kind="ExternalInput").ap()
        return self.dr[name]

    def dout(self, name, shape, dt=F32):
        self.dr[name] = self.nc.dram_tensor(name, list(shape), dt, kind="ExternalOutput").ap()
        return self.dr[name]

    def init_psum(self):
        nc = self.nc
        self.banks = [nc.alloc_psum_tensor(f"bank{i}", [128, 512], F32) for i in range(8)]
        self.h_i = 0
        self.f_i = 0
        self.qh_i = 0
        self.q_i = 0

    def ph(self, n=256):
        i = self.h_i % 8
        self.h_i += 1
        b, off = i % 4, (i // 4) * 256
        return T(self.banks[b][:, off:off + n], "ps", b * 2048 + off * 4, n * 4, [128, n], F32)

    def pf(self):
        b = self.f_i % 8
        self.f_i += 1
        return T(self.banks[b][:, 0:512], "ps", b * 2048, 2048, [128, 512], F32)

    def pqh(self):
        i = self.qh_i % 8
        self.qh_i += 1
        b, off = 4 + i % 4, (i // 4) * 256
        return T(self.banks[b][:, off:off + 256], "ps", b * 2048 + off * 4, 1024, [128, 256], F32)

    def pq(self, n=128):
        i = self.q_i % 16
        self.q_i += 1
        b, off = 4 + i % 4, (i // 4) * 128
        return T(self.banks[b][:, off:off + n], "ps", b * 2048 + off * 4, n * 4, [128, n], F32)

    def act(self, out, in_, func, reads, writes, bias=None, scale=None):
        kw = {}
        if bias is not None:
            kw["bias"] = bias
        if scale is not None:
            kw["scale"] = scale
        self.s.op("act", lambda e: e.activation(out, in_, func, **kw), reads, writes)

    def tt(self, eng, out, a, b, op, reads, writes):
        self.s.op(eng, lambda e: e.tensor_tensor(out, a, b, op), reads, writes)

    def stt(self, out, in0, scalar, in1, op0, op1, reads, writes):
        self.s.op("dve", lambda e: e.scalar_tensor_tensor(out, in0, scalar, in1, op0, op1), reads, writes)

    def ts(self, eng, out, in0, s1, op0, reads, writes, s2=None, op1=None):
        if op1 is None:
            self.s.op(eng, lambda e: e.tensor_scalar(out, in0, s1, None, op0), reads, writes)
        else:
            self.s.op(eng, lambda e: e.tensor_scalar(out, in0, s1, s2, op0, op1), reads, writes)

    def cp(self, eng, out, in_, reads, writes):
        if eng == "act":
            self.s.op("act", lambda e: e.copy(out, in_), reads, writes)
        else:
            self.s.op(eng, lambda e: e.tensor_copy(out, in_), reads, writes)

    def mm(self, out_t, out_ap, pairs, reads):
        def fn(e):
            n = len(pairs)
            ins = None
            for i, (l, r) in enumerate(pairs):
                ins = e.matmul(out_ap, l, r, start=(i == 0), stop=(i == n - 1))
            return ins
        self.s.op("pe", fn, reads, [out_t])

    def ld(self, eng, key, out_t, out_ap, in_ap, extra_w=()):
        self.s.dma(eng, key, lambda e: e.dma_start(out=out_ap, in_=in_ap), [], [out_t] + list(extra_w))

    def st(self, eng, key, out_ap, in_t, in_ap):
        self.s.dma(eng, key, lambda e: e.dma_start(out=out_ap, in_=in_ap), [in_t], [])

    def build(self):
        s = self.s
        nc = self.nc
        xT = self.din("xT", [128, 8, NTOK])
        w_in = self.din("w_in", [2, 128, 8, 3072])
        w_ab = self.din("w_ab", [2, 128, 8, 8])
        w_out = self.din("w_out", [2, 128, 8, 1024])
        w_gate = self.din("w_gate", [2, 128, 8, DFF])
        w_up = self.din("w_up", [2, 128, 8, DFF])
        w_down = self.din("w_down", [2, 128, 22, 1024])
        cst = self.din("cst", [128, 17, 128])
        segsel = self.din("segsel", [128, 16, 128])
        prm = self.din("prm", [128, 2, 256])
        fnorm = self.din("fnorm", [128, 8])
        rc16 = self.din("rc16", [128, 2, 16])
        st_delta = self.din("st_delta", [2, 4, 128, 16, 128])
        st_dconv = self.din("st_dconv", [2, 128, 12, 16, 3])
        st_pool = self.din("st_pool", [2, 128, 2, 16, 15])
        st_sconv = self.din("st_sconv", [2, 128, 2, 16, 2])
        st_fconv = self.din("st_fconv", [2, 128, 22, 16, 2])
        yT = self.dout("yT", [128, 8, NTOK])
        o_pdelta = self.dout("o_pdelta", [2, 128, 4, 128])
        o_pdconv = self.dout("o_pdconv", [2, 128, 12, 3])
        o_ppool = self.dout("o_ppool", [2, 128, 2, 15])
        o_psconv = self.dout("o_psconv", [2, 128, 2, 2])
        o_pfconv = self.dout("o_pfconv", [2, 128, 22, 2])
        o_sdelta = self.dout("o_sdelta", [2, 4, 128, 16, 128])
        o_sdconv = self.dout("o_sdconv", [2, 128, 12, 16, 3])
        o_spool = self.dout("o_spool", [2, 128, 2, 16, 15])
        o_ssconv = self.dout("o_ssconv", [2, 128, 2, 16, 2])
        o_sfconv = self.dout("o_sfconv", [2, 128, 22, 16, 2])

        self.init_psum()
        C = s.sb("cst", [128, 17, 128], F32)
        SEG8B = s.sb("seg8b", [128, 128], BF16)
        MSKB = s.sb("mskb", [128, 4, 128], BF16, addr=C.addr + 5 * 512)
        Cb = s.sb("cstb", [128, 128], BF16)
        SEGSEL = s.sb("segsel", [128, 16, 128], BF16)
        PRM = s.sb("prm", [128, 2, 256], F32)
        FN = s.sb("fnorm", [128, 8], F32)
        RC = s.sb("rc16", [128, 2, 16], F32)
        EPSC = s.sb("epsc", [128, 4], F32)
        self.ld("sp", "c0", C, C[:], cst)
        self.ld("pool", "c1", SEGSEL, SEGSEL[:], segsel)
        self.ld("sp", "c2", PRM, PRM[:], prm)
        self.ld("sp", "c3", FN, FN[:], fnorm)
        self.ld("sp", "c4", RC, RC[:], rc16)
        s.op("pool", lambda e: e.memset(EPSC[:, 0:1], EPS), [], [EPSC])
        s.op("pool", lambda e: e.memset(EPSC[:, 1:2], 128.0 * EPS), [], [EPSC])
        self.cp("pool", Cb[:], C[:, 0, :], [C], [Cb])
        self.cp("pool", SEG8B[:], C[:, 4, :], [C], [SEG8B])
        self.ld("pool", "mskb", MSKB, MSKB[:], cst[:, 5:9, :])
        IDF, ONES = C[:, 0, :], C[:, 1, :]
        IDB = Cb[:]
        P_NM, P_NF, P_DCW, P_ONORM, P_PSC, P_SCW, P_FCW, P_DTB, P_ALOG, P_IW = 0, 8, 16, 64, 65, 67, 73, 139, 143, 147
        CS = [s.sb(f"cS{l}", [128, 4, 128], F32) for l in range(2)]
        CSb = [s.sb(f"cSb{l}", [128, 4, 128], BF16) for l in range(2)]
        CDC = [s.sb(f"cdc{l}", [128, 12, 3], F32) for l in range(2)]
        CPL = [s.sb(f"cpl{l}", [128, 2, 15], F32) for l in range(2)]
        CSC = [s.sb(f"csc{l}", [128, 2, 2], F32) for l in range(2)]
        CFC = [s.sb(f"cfc{l}", [128, 22, 2], F32) for l in range(2)]
        NEXPA = [s.sb(f"nexpa{l}", [128, 4], F32) for l in range(2)]
        for l in range(2):
            for t in (CS[l], CSb[l], CDC[l], CPL[l], CSC[l], CFC[l]):
                s.op("pool", (lambda t: (lambda e: e.memset(t[:], 0.0)))(t), [], [t])
            self.act(NEXPA[l][:], PRM[:, l, P_ALOG:P_ALOG + 4], AF.Exp, [PRM], [NEXPA[l]])
            self.ts("dve", NEXPA[l][:], NEXPA[l][:], -1.0, ALU.mult, [NEXPA[l]], [NEXPA[l]])
        XA = s.sb("xa", [128, 8, MAXT * NT], F32)
        HFA = s.sb("hfa", [128, 8, MAXT * NT], BF16)

        def colview(base, lo, ncols, esz):
            t = T(base.ap[:, :, lo:lo + ncols], "sb", base.addr, base.nbytes, [128, 8, ncols], base.dtype)
            row = MAXT * NT * esz
            t.subl = [("sb", base.addr + kc * row + lo * esz, base.addr + kc * row + (lo + ncols) * esz) for kc in range(8)]
            t.ivl = list(t.subl)
            return t

        X = [colview(XA, i * NT, NT, 4) for i in range(MAXT)]
        HF = [colview(HFA, i * NT, NT, 2) for i in range(MAXT)]
        w0 = s.sb_ptr
        WIN = [s.sb(f"win{j}", [128, 8, 512], BF16) for j in range(6)]
        WOUT = [s.sb(f"wout{j}", [128, 8, 512], BF16) for j in range(2)]
        WAB = s.sb("wab", [128, 8, 8], BF16)
        wend = s.sb_ptr
        a = w0
        FPASS = [(0, 6), (6, 5), (11, 6), (17, 5)]
        WGr, WUr, WDr = {}, {}, {}
        for reg, nfr in (("A", 6), ("B", 5)):
            WGr[reg] = s.sb("wg" + reg, [128, 8, nfr * 128], BF16, addr=a); a += 8 * nfr * 128 * 2
            WUr[reg] = s.sb("wu" + reg, [128, 8, nfr * 128], BF16, addr=a); a += 8 * nfr * 128 * 2
            WDr[reg] = s.sb("wd" + reg, [128, nfr, 1024], BF16, addr=a); a += nfr * 1024 * 2
        s.sb_ptr = max(wend, a)
        wk0 = s.sb_ptr
        ht_addr = s.sb_ptr
        HT = s.sb("ht", [128, 8, NT], BF16)
        ssum_addr = s.sb_ptr
        SSUM = s.sb("ssum", [128, NT], F32)
        eq_addr = s.sb_ptr
        EQ = s.sb("extqkv", [128, 12, 3 + NT], F32)
        CV = [s.sb(f"cv{i}", [128, NT], F32) for i in range(2)]
        CV2 = CV
        TMPA = [s.sb(f"tmpa{i}", [128, NT], F32) for i in range(2)]
        QT = s.sb("qT", [128, 4, NT], BF16)
        KT = s.sb("kT", [128, 4, NT], BF16)
        VT = s.sb("vT", [128, 4, NT], F32)
        zs_addr = s.sb_ptr
        ZS = s.sb("zs", [128, 4, NT], F32)
        OT = s.sb("oT", [128, 4, NT], F32)
        SQ8 = s.sb("sq8", [128, 8, NT], F32, addr=zs_addr)
        RSTD = s.sb("rstd", [128, NT], F32, addr=zs_addr)
        EP = s.sb("extp", [128, 2, 368], F32)
        ps2_addr = s.sb_ptr
        PS2 = s.sb("ps2", [128, 368], F32)
        ps4_addr = s.sb_ptr
        PS4 = s.sb("ps4", [128, 368], F32)
        PS8 = s.sb("ps8", [128, 368], F32, addr=ps2_addr)
        PS16 = s.sb("ps16", [128, 368], F32, addr=ps4_addr)
        pd_addr = s.sb_ptr
        PD = s.sb("pd", [128, NT], BF16)
        SCB = s.sb("scb", [128, 2, NT], F32)
        ES = s.sb("exts", [128, 2, 2 + NT], F32)
        ocat_addr = s.sb_ptr
        OCAT = s.sb("ocat", [128, 8, NT], BF16)
        SM = [s.sb(f"sm{i}", [128, 48], F32) for i in range(2)]
        NU = 4
        U2 = []
        for pp in range(2):
            d = {}
            for nm in ("D", "DT"):
                d[nm] = s.sb(f"u{nm}{pp}", [128, 2, 128], F32)
            d["VB"] = [s.sb(f"uVB{pp}{i}", [128, 2, 128], F32) for i in range(2)]
            for nm in ("NA", "X0", "XTa", "Xa", "IXT", "Pa", "Pb", "R", "VN", "KD", "NAD", "X0D", "Ta", "Tb"):
                d[nm] = s.sb(f"u{nm}{pp}", [128, 2, 128], BF16)
            for nm in ("QKD", "QG", "PF"):
                d[nm] = [s.sb(f"u{nm}{pp}{i}", [128, 2, 128], BF16) for i in range(2)]
            d["P0"] = d["Pb"]
            d["M1"] = d["XTa"]
            d["M2"] = d["Xa"]
            d["EGL"] = [s.sb(f"uegl{pp}{i}", [128, 2, 16], F32) for i in range(2)]
            U2.append(d)

        def hview(t, j):
            per = t.nbytes // 2
            return T(t.ap[:, j], "sb", t.addr + j * per, per, [128] + t.shape[2:], t.dtype)

        U = []
        for h in range(4):
            d2 = U2[h // 2]
            d = {}
            for k, v in d2.items():
                d[k] = [hview(x, h % 2) for x in v] if isinstance(v, list) else hview(v, h % 2)
            U.append(d)
        KTZ = s.sb("ktz", [128, 16, 128], BF16, addr=ht_addr)
        KDZ = [s.sb(f"kdz{i}", [128, 128], BF16, addr=ssum_addr + 256 * i) for i in range(4)]
        SSF = s.sb("ssf", [128, 16, 128], F32, addr=eq_addr)
        SSB = s.sb("ssb", [128, 16, 128], BF16, addr=eq_addr + 8192)
        STG_DC = s.sb("stgdc", [128, 12, 16, 3], F32, addr=ocat_addr)
        STG_PL = s.sb("stgpl", [128, 2, 16, 15], F32, addr=ps2_addr)
        STG_SC = s.sb("stgsc", [128, 2, 16, 2], F32, addr=pd_addr)
        wkM_end = s.sb_ptr
        s.sb_ptr = wk0
        NTF = 512
        EF = s.sb("extf", [128, 6, 2 + NTF], F32)
        ACTT = s.sb("actt", [128, 6, NTF], BF16)
        FT = [s.sb(f"ft{i}", [128, NTF], F32) for i in range(3)]
        SQ8f = s.sb("sq8f", [128, 8, NTF], F32)
        SSUMf = s.sb("ssumf", [128, NTF], F32)
        RSTDf = s.sb("rstdf", [128, NTF], F32)
        YST = s.sb("yst", [128, 8, NTF], F32)
        STG_FC = s.sb("stgfc", [128, 6, 16, 2], F32)
        s.sb_ptr = max(wkM_end, s.sb_ptr)
        self.sb_used = s.sb_ptr

        def rmsnorm(xt, n, gam_ap, outs, sq8, ssum, rstd, out_reads_extra=()):
            self.act(sq8[:, :, 0:n], xt[:, :, 0:n], AF.Square, [xt], [sq8])
            s.op("dve", lambda e: e.tensor_reduce(ssum[:, 0:n], sq8[:, :, 0:n].rearrange("p k n -> p n k"), AX.X, ALU.add),
                 [sq8], [ssum])
            ps = self.pf() if n > 256 else self.ph()
            self.mm(ps, ps[:, 0:n], [(ONES, ssum[:, 0:n])], [C, ssum])
            self.act(rstd[:, 0:n], ps[:, 0:n], AF.Ln, [ps, EPSC], [rstd], bias=EPSC[:, 0:1], scale=1.0 / D)
            self.act(rstd[:, 0:n], rstd[:, 0:n], AF.Exp, [rstd], [rstd], scale=-0.5)
            ot, ofn = outs
            for kc in range(8):
                self.stt(ofn(kc), xt[:, kc, 0:n], gam_ap[:, kc:kc + 1], rstd[:, 0:n], ALU.mult, ALU.mult,
                         [xt, PRM, FN, rstd], [ot])

        def conv_taps(eng0, out_t, out3, ext_t, ext3, wcol, ntap, tmp_t=None):
            if eng0 == "act":
                self.act(out3, ext3(0), AF.Copy, [ext_t, PRM], [out_t], scale=wcol(0))
            else:
                self.ts(eng0, out3, ext3(0), wcol(0), ALU.mult, [ext_t, PRM], [out_t])
            for i in range(1, ntap):
                self.stt(out3, ext3(i), wcol(i), out3, ALU.mult, ALU.add, [ext_t, PRM, out_t], [out_t])

        def loads_M(l, which):
            for j in which:
                if j < 6:
                    self.ld("pool", f"win{j}", WIN[j], WIN[j][:], w_in[l, :, :, j * 512:(j + 1) * 512])
                elif j < 8:
                    self.ld("pool", f"wout{j - 6}", WOUT[j - 6], WOUT[j - 6][:], w_out[l, :, :, (j - 6) * 512:(j - 5) * 512])
                else:
                    self.ld("pool", "wab", WAB, WAB[:], w_ab[l])

        def loads_F(l, p):
            f0, nf = FPASS[p]
            reg = "A" if nf == 6 else "B"
            cs = slice(f0 * 128, (f0 + nf) * 128)
            self.ld("pool", "wg" + reg, WGr[reg], WGr[reg][:], w_gate[l, :, :, cs])
            self.ld("pool", "wu" + reg, WUr[reg], WUr[reg][:], w_up[l, :, :, cs])
            self.ld("pool", "wd" + reg, WDr[reg], WDr[reg][:], w_down[l, :, f0:f0 + nf, :])

        def stage_M(l, tiles, start_loads, early_next):
            loads_M(l, start_loads)
            prm_l = PRM[:, l, :]
            for ti, (c0, n, nseg, L) in enumerate(tiles):
                is_s = nseg > 1
                xt = X[ti]

                def v3(ap2):
                    return ap2.rearrange("p (s t) -> p s t", s=nseg)

                if ti == 0:
                    rmsnorm(xt, n, prm_l[:, P_NM:P_NM + 8], (HT, lambda kc: HT[:, kc, 0:n]), SQ8, SSUM, RSTD)
                s.mark(f'M{l} t{ti} norm done')
                if is_s:
                    self.ld("sp", "sdc", STG_DC, STG_DC[:], st_dconv[l])
                    self.ld("sp", "spl", STG_PL, STG_PL[:], st_pool[l])
                    self.ld("sp", "ssc", STG_SC, STG_SC[:], st_sconv[l])
                eq4 = EQ[:, :, 0:nseg * (3 + L)].rearrange("p c (s t) -> p c s t", s=nseg)
                ep4 = EP[:, :, 0:nseg * (15 + L)].rearrange("p c (s t) -> p c s t", s=nseg)
                es4 = ES[:, :, 0:nseg * (2 + L)].rearrange("p c (s t) -> p c s t", s=nseg)
                if is_s:
                    self.cp("pool", eq4[:, :, :, 0:3], STG_DC[:], [STG_DC], [EQ])
                    self.cp("pool", ep4[:, :, :, 0:15], STG_PL[:], [STG_PL], [EP])
                    self.cp("pool", es4[:, :, :, 0:2], STG_SC[:], [STG_SC], [ES])
                else:
                    self.cp("pool", eq4[:, :, 0, 0:3], CDC[l][:], [CDC[l]], [EQ])
                    self.cp("pool", ep4[:, :, 0, 0:15], CPL[l][:], [CPL[l]], [EP])
                    self.cp("pool", es4[:, :, 0, 0:2], CSC[l][:], [CSC[l]], [ES])
                s.mark(f'M{l} t{ti} hist init done')
                def proj(c):
                    ps = self.ph()
                    wt = WIN[c // 4]
                    co = (c % 4) * 128
                    self.mm(ps, ps[:, 0:n], [(wt[:, kc, co:co + 128], HT[:, kc, 0:n]) for kc in range(8)], [wt, HT])
                    if c < 12:
                        self.act(eq4[:, c, :, 3:3 + L], v3(ps[:, 0:n]), AF.Copy, [ps], [EQ.sub(c)])
                    elif c < 16:
                        self.act(ZS[:, c - 12, 0:n], ps[:, 0:n], AF.Silu, [ps], [ZS])
                    elif c < 18:
                        self.act(ep4[:, c - 16, :, 15:15 + L], v3(ps[:, 0:n]), AF.Copy, [ps], [EP])
                    elif c < 20:
                        self.act(TMPA[c - 18][:, 0:n], ps[:, 0:n], AF.Copy, [ps], [TMPA[c - 18]])
                    elif c < 22:
                        self.act(SCB[:, c - 20, 0:n], ps[:, 0:n], AF.Copy, [ps], [SCB])
                    else:
                        self.tt("dve", es4[:, c - 22, :, 2:2 + L], v3(ps[:, 0:n]), v3(TMPA[c - 22][:, 0:n]), ALU.mult,
                                [ps, TMPA[c - 22]], [ES])

                for c in range(12):
                    proj(c)
                if is_s:
                    self.cp("pool", STG_DC[:], eq4[:, :, :, L:L + 3], [EQ], [STG_DC])
                    self.st("sp", "osdc", o_sdconv[l], STG_DC, STG_DC[:])
                else:
                    self.cp("pool", CDC[l][:], eq4[:, :, 0, L:L + 3], [EQ], [CDC[l]])
                    if c0 + n == NPROMPT:
                        self.st("sp", "opdc", o_pdconv[l], CDC[l], CDC[l][:])
                def cs_ap(c):
                    return eq4[:, c, :, 3:3 + L]

                def convA1(c):
                    cv = CV[c % 2]
                    conv_taps("act", cv, v3(cv[:, 0:n]), EQ.sub(c), lambda i, c=c: eq4[:, c, :, i:i + L],
                              lambda i, c=c: prm_l[:, P_DCW + c * 4 + i:P_DCW + c * 4 + i + 1], 4)

                def convA2(c):
                    cv = CV[c % 2]
                    if c < 8:
                        self.act(cs_ap(c), v3(cv[:, 0:n]), AF.Silu, [cv], [EQ.sub(c)])
                    else:
                        self.act(VT[:, c % 4, 0:n], cv[:, 0:n], AF.Silu, [cv], [VT])

                bps = {}

                def convB1(c):
                    sq = TMPA[c % 2]
                    self.act(v3(sq[:, 0:n]), cs_ap(c), AF.Square, [EQ.sub(c)], [sq])
                    ps = self.ph()
                    self.mm(ps, ps[:, 0:n], [(ONES, sq[:, 0:n])], [C, sq])
                    bps[c] = ps

                def convB2(c):
                    h = c % 4
                    rn = CV2[c % 2]
                    ps = bps.pop(c)
                    if c < 4:
                        self.act(rn[:, 0:n], ps[:, 0:n], AF.Ln, [ps, EPSC], [rn], bias=EPSC[:, 1:2], scale=128.0)
                    else:
                        self.act(rn[:, 0:n], ps[:, 0:n], AF.Ln, [ps, EPSC], [rn], bias=EPSC[:, 0:1], scale=1.0)
                    self.act(rn[:, 0:n], rn[:, 0:n], AF.Exp, [rn], [rn], scale=-0.5)
                    dst = QT if c < 4 else KT
                    self.tt("pool", v3(dst[:, h, 0:n]), cs_ap(c), v3(rn[:, 0:n]), ALU.mult, [EQ.sub(c), rn], [dst])

                def convA_gen():
                    for c in range(13):
                        if c < 12:
                            convA1(c)
                        if c >= 1:
                            convA2(c - 1)
                        yield

                def convB_gen():
                    for c in range(9):
                        if c < 8:
                            convB1(c)
                        if c >= 1:
                            convB2(c - 1)
                        yield

                def rr(gens):
                    gens = list(gens)
                    while gens:
                        nxt = []
                        for g in gens:
                            try:
                                next(g)
                                nxt.append(g)
                            except StopIteration:
                                pass
                        gens = nxt

                def proj_rest_gen():
                    for c in range(12, 24):
                        proj(c)
                        yield

                rr([proj_rest_gen(), convA_gen()])
                s.mark(f'M{l} t{ti} proj done')
                if ti == len(tiles) - 1:
                    early_next()
                if is_s:
                    self.cp("pool", STG_PL[:], ep4[:, :, :, L:L + 15], [EP], [STG_PL])
                    self.cp("pool", STG_SC[:], es4[:, :, :, L:L + 2], [ES], [STG_SC])
                    self.st("sp", "ospl", o_spool[l], STG_PL, STG_PL[:])
                    self.st("sp", "ossc", o_ssconv[l], STG_SC, STG_SC[:])
                else:
                    self.cp("pool", CPL[l][:], ep4[:, :, 0, L:L + 15], [EP], [CPL[l]])
                    self.cp("pool", CSC[l][:], es4[:, :, 0, L:L + 2], [ES], [CSC[l]])
                    if c0 + n == NPROMPT:
                        self.st("sp", "oppl", o_ppool[l], CPL[l], CPL[l][:])
                        self.st("sp", "opsc", o_psconv[l], CSC[l], CSC[l][:])
                TRI = C[:, 3, :] if is_s else C[:, 2, :]
                POSLb = MSKB[:, 1, :] if is_s else MSKB[:, 0, :]
                NEGUb = MSKB[:, 3, :] if is_s else MSKB[:, 2, :]
                SEG = C[:, 4, :] if is_s else ONES
                POSL = C[:, 6, :] if is_s else C[:, 5, :]
                NEGU = C[:, 8, :] if is_s else C[:, 7, :]

                def small_gen(ck):
                    cols = slice(ck * 128, ck * 128 + 128)
                    sm = SM[ck % 2]
                    pab = self.pq()
                    self.mm(pab, pab[:, 0:8], [(HT[:, kc, cols], WAB[:, kc, :]) for kc in range(8)], [HT, WAB])
                    self.tt("dve", sm[:, 0:4], pab[:, 0:4], prm_l[:, P_DTB:P_DTB + 4], ALU.add, [pab, PRM], [sm])
                    self.act(sm[:, 16:20], pab[:, 4:8], AF.Exp, [pab], [sm], scale=-1.0)
                    yield
                    self.act(sm[:, 4:8], sm[:, 0:4], AF.Exp, [sm], [sm])
                    self.ts("dve", sm[:, 16:20], sm[:, 16:20], 1.0, ALU.add, [sm], [sm])
                    yield
                    self.act(sm[:, 8:12], sm[:, 4:8], AF.Ln, [sm], [sm], bias=1.0, scale=1.0)
                    s.op("dve", (lambda sm: (lambda e: e.reciprocal(sm[:, 20:24], sm[:, 16:20])))(sm), [sm], [sm])
                    yield
                    self.tt("dve", sm[:, 12:16], sm[:, 8:12], NEXPA[l][:], ALU.mult, [sm, NEXPA[l]], [sm])
                    self.ts("dve", sm[:, 44:48], sm[:, 20:24], -1.0, ALU.mult, [sm], [sm])
                    yield
                    pgc = self.pq()
                    self.mm(pgc, pgc[:, 0:4], [(TRI, sm[:, 12:16])], [C, sm])
                    pgl = self.pq()
                    self.mm(pgl, pgl[:, 0:4], [(SEG, sm[:, 12:16])], [C, sm])
                    self.cp("dve", sm[:, 24:28], pgc[:, 0:4], [pgc], [sm])
                    self.tt("dve", sm[:, 40:44], pgl[:, 0:4], sm[:, 24:28], ALU.subtract, [pgl, sm], [sm])
                    yield
                    self.ts("dve", sm[:, 28:32], sm[:, 24:28], -1.0, ALU.mult, [sm], [sm])
                    self.act(sm[:, 32:36], sm[:, 24:28], AF.Exp, [sm], [sm])
                    self.act(sm[:, 40:44], sm[:, 40:44], AF.Exp, [sm], [sm])
                    yield
                    self.stt(sm[:, 36:40], sm[:, 32:36], -1.0, sm[:, 20:24], ALU.mult, ALU.mult, [sm], [sm])

                rr([convB_gen()] + [small_gen(ck) for ck in range(n // 128)])
                s.mark(f'M{l} t{ti} conv done')
                def mixers_gen():
                    WP = nseg * (15 + L)
                    for pc in range(2):
                        e3 = ep4[:, pc]

                        d2 = PS2[:, 0:WP].rearrange("p (s t) -> p s t", s=nseg)
                        d4 = PS4[:, 0:WP].rearrange("p (s t) -> p s t", s=nseg)
                        d8 = PS8[:, 0:WP].rearrange("p (s t) -> p s t", s=nseg)
                        d16 = PS16[:, 0:WP].rearrange("p (s t) -> p s t", s=nseg)
                        W = 15 + L
                        self.tt("pool", d2[:, :, 1:W], e3[:, :, 1:W], e3[:, :, 0:W - 1], ALU.add, [EP], [PS2])
                        self.tt("pool", d4[:, :, 3:W], d2[:, :, 3:W], d2[:, :, 1:W - 2], ALU.add, [PS2], [PS4])
                        yield
                        if pc == 0:
                            lo_src, hi_src = d2, d4
                            lo_t, hi_t = PS2, PS4
                        else:
                            self.tt("pool", d8[:, :, 7:W], d4[:, :, 7:W], d4[:, :, 3:W - 4], ALU.add, [PS4], [PS8])
                            self.tt("pool", d16[:, :, 15:W], d8[:, :, 15:W], d8[:, :, 7:W - 8], ALU.add, [PS8], [PS16])
                            yield
                            lo_src, hi_src = d8, d16
                            lo_t, hi_t = PS8, PS16
                        pd3 = PD[:, 0:n].rearrange("p (s t) -> p s t", s=nseg)
                        iw = prm_l[:, P_IW + pc:P_IW + pc + 1]
                        for (half, src, srct) in ((slice(0, 64), lo_src, lo_t), (slice(64, 128), hi_src, hi_t)):
                            self.stt(pd3[half], src[half, :, 15:W], iw[half], e3[half, :, 15:W], ALU.mult, ALU.subtract,
                                     [srct, PRM, EP], [PD])
                            if (not is_s) and c0 == 0:
                                tf = TMPA[0]
                                self.tt("dve", tf[half, 0:16], src[half, 0, 15:31], RC[half, pc, :], ALU.mult, [srct, RC], [tf])
                                self.tt("dve", PD[half, 0:16], tf[half, 0:16], e3[half, 0, 15:31], ALU.subtract, [tf, EP], [PD])
                        yield
                        ps = self.ph()
                        self.mm(ps, ps[:, 0:n], [(PWB[l][:, pc, :], PD[:, 0:n])], [PWB[l], PD])
                        self.act(OCAT[:, 4 + pc, 0:n], ps[:, 0:n], AF.Copy, [ps, PRM], [OCAT],
                                 scale=prm_l[:, P_PSC + pc:P_PSC + pc + 1])
                    s.mark(f'M{l} t{ti} pool done')
                    for c in range(2):
                        cv = CV[c % 2]
                        conv_taps("act", cv, v3(cv[:, 0:n]), ES, lambda i, c=c: es4[:, c, :, i:i + L],
                                  lambda i, c=c: prm_l[:, P_SCW + c * 3 + i:P_SCW + c * 3 + i + 1], 3)
                        self.tt("pool", OCAT[:, 6 + c, 0:n], cv[:, 0:n], SCB[:, c, 0:n], ALU.mult, [cv, SCB], [OCAT])
                        yield

                def mk(ck):
                    cols = slice(ck * 128, ck * 128 + 128)
                    sm = SM[ck % 2]
                    fin = {}
                    par = ck % 2

                    def unit_pre2(pp):
                        u2 = U2[pp]
                        hs = (2 * pp, 2 * pp + 1)
                        us = [U[h] for h in hs]
                        I2 = IDF.unsqueeze(1).to_broadcast([128, 2, 128])
                        js = [slice(0, 128), slice(128, 256)]

                        def v2(pt):
                            return pt[:].rearrange("p (j n) -> p j n", j=2)

                        kT = [KT[:, h, cols] for h in hs]
                        qT = [QT[:, h, cols] for h in hs]
                        vT = [VT[:, h, cols] for h in hs]
                        P1, P2 = self.pqh(), self.pqh()
                        for j, h in enumerate(hs):
                            gbc = sm[:, 12 + h:13 + h].to_broadcast([128, 128])
                            self.mm(P1, P1[:, js[j]], [(gbc, TRI), (IDB, POSLb)], [sm, C, Cb, MSKB])
                            self.mm(P2, P2[:, js[j]], [(gbc, TRI), (IDB, NEGUb)], [sm, C, Cb, MSKB])
                        for j, h in enumerate(hs):
                            self.act(us[j]["D"][:], P1[:, js[j]], AF.Exp, [P1, sm], [us[j]["D"]], bias=sm[:, 24 + h:25 + h], scale=-1.0)
                            self.act(us[j]["DT"][:], P2[:, js[j]], AF.Exp, [P2, sm], [us[j]["DT"]], bias=sm[:, 28 + h:29 + h], scale=1.0)
                        yield
                        PKK, PQK = self.pqh(), self.pqh()
                        for j, h in enumerate(hs):
                            self.mm(PKK, PKK[:, js[j]], [(kT[j], kT[j])], [KT])
                            self.mm(PQK, PQK[:, js[j]], [(kT[j], qT[j])], [KT, QT])
                        for j, h in enumerate(hs):
                            self.stt(us[j]["NA"][:], PKK[:, js[j]], sm[:, 44 + h:45 + h], us[j]["D"][:], ALU.mult, ALU.mult,
                                     [PKK, sm, us[j]["D"]], [us[j]["NA"]])
                        self.tt("dve", u2["QKD"][par][:], v2(PQK), u2["DT"][:], ALU.mult, [PQK, u2["DT"]], [u2["QKD"][par]])
                        PV = self.pqh()
                        for j, h in enumerate(hs):
                            self.mm(PV, PV[:, js[j]], [(vT[j], IDF)], [VT, C])
                        for j, h in enumerate(hs):
                            self.act(us[j]["VB"][par][:], PV[:, js[j]], AF.Copy, [PV, sm], [us[j]["VB"][par]], scale=sm[:, 20 + h:21 + h])
                        yield
                        P3 = self.pqh()
                        for j, h in enumerate(hs):
                            self.mm(P3, P3[:, js[j]], [(sm[:, 32 + h:33 + h].to_broadcast([128, 128]), IDF)], [sm, C])
                        self.tt("dve", u2["QG"][par][:], v2(P3), QT[:, hs[0]:hs[0] + 2, cols], ALU.mult, [P3, QT], [u2["QG"][par]])
                        if is_s:
                            self.cp("act", u2["EGL"][par][:, :, 0:16],
                                    P3[:].rearrange("p (j s t) -> p j s t", j=2, t=8)[:, :, :, 7], [P3], [u2["EGL"][par]])
                        else:
                            self.cp("act", u2["EGL"][par][:, :, 0:1], v2(P3)[:, :, 127:128], [P3], [u2["EGL"][par]])
                        PBT = self.pqh()
                        for j, h in enumerate(hs):
                            self.mm(PBT, PBT[:, js[j]], [(us[j]["NA"][:], IDB)], [us[j]["NA"], Cb])
                        self.cp("act", u2["X0"][:], v2(PBT), [PBT], [u2["X0"]])
                        if is_s:
                            NAd, X0d = u2["NA"], u2["X0"]
                            self.tt("dve", u2["P0"][:], v2(PBT), I2, ALU.add, [PBT, C], [u2["P0"]])
                        else:
                            NAd, X0d = u2["NAD"], u2["X0D"]
                            self.tt("pool", NAd[:], u2["NA"][:], SEG8B[:].unsqueeze(1).to_broadcast([128, 2, 128]), ALU.mult,
                                    [u2["NA"], SEG8B], [NAd])
                        yield
                        if not is_s:
                            PBD = self.pqh()
                            for j in range(2):
                                self.mm(PBD, PBD[:, js[j]], [(NAd[:, j, :], IDB)], [NAd, Cb])
                            self.cp("act", X0d[:], v2(PBD), [PBD], [X0d])
                            self.tt("dve", u2["P0"][:], v2(PBD), I2, ALU.add, [PBD, C], [u2["P0"]])
                            yield
                        Xc, XTc, Pc, Tc = X0d, NAd, u2["P0"], None
                        for m in range(1, 3):
                            last = m == 2
                            PXT = self.pqh()
                            for j in range(2):
                                self.mm(PXT, PXT[:, js[j]], [(Xc[:, j, :], XTc[:, j, :])], [Xc, XTc])
                            if not last:
                                PXX = self.pqh()
                                for j in range(2):
                                    self.mm(PXX, PXX[:, js[j]], [(XTc[:, j, :], Xc[:, j, :])], [Xc, XTc])
                                XTn, Xn = u2["XTa"], u2["Xa"]
                                self.cp("dve", XTn[:], v2(PXT), [PXT], [XTn])
                                self.cp("act", Xn[:], v2(PXX), [PXX], [Xn])
                            self.tt("dve", u2["IXT"][:], v2(PXT), I2, ALU.add, [PXT, C], [u2["IXT"]])
                            yield
                            PP = self.pqh()
                            for j in range(2):
                                self.mm(PP, PP[:, js[j]], [(u2["IXT"][:, j, :], Pc[:, j, :])], [u2["IXT"], Pc])
                            Pn = u2["Pa"] if m % 2 else u2["Pb"]
                            if is_s and last:
                                Pn = u2["PF"][par]
                            self.cp("act", Pn[:], v2(PP), [PP], [Pn])
                            if not (is_s and last):
                                PT = self.pqh()
                                for j in range(2):
                                    self.mm(PT, PT[:, js[j]], [(Pc[:, j, :], u2["IXT"][:, j, :])], [u2["IXT"], Pc])
                                Tn = u2["Ta"] if m % 2 else u2["Tb"]
                                self.cp("act", Tn[:], v2(PT), [PT], [Tn])
                                Tc = Tn
                            Pc = Pn
                            if not last:
                                Xc, XTc = Xn, XTn
                            yield
                        if not is_s:
                            for lev in range(4):
                                Wm = C[:, 9 + lev, :].unsqueeze(1).to_broadcast([128, 2, 128])
                                WTm = C[:, 13 + lev, :].unsqueeze(1).to_broadcast([128, 2, 128])
                                PM1 = self.pqh()
                                for j in range(2):
                                    self.mm(PM1, PM1[:, js[j]], [(u2["NA"][:, j, :], Pc[:, j, :])], [u2["NA"], Pc])
                                self.tt("dve", u2["M1"][:], v2(PM1), Wm, ALU.mult, [PM1, C], [u2["M1"]])
                                if lev < 3:
                                    PM2 = self.pqh()
                                    for j in range(2):
                                        self.mm(PM2, PM2[:, js[j]], [(u2["X0"][:, j, :], Tc[:, j, :])], [u2["X0"], Tc])
                                    self.tt("dve", u2["M2"][:], v2(PM2), WTm, ALU.mult, [PM2, C], [u2["M2"]])
                                yield
                                PPN = self.pqh()
                                for j in range(2):
                                    self.mm(PPN, PPN[:, js[j]], [(Tc[:, j, :], u2["M1"][:, j, :])], [Tc, u2["M1"]])
                                Pn = u2["Pa"] if Pc is u2["Pb"] else u2["Pb"]
                                if lev == 3:
                                    Pn = u2["PF"][par]
                                self.tt("dve", Pn[:], v2(PPN), Pc[:], ALU.add, [PPN, Pc], [Pn])
                                if lev < 3:
                                    PTN = self.pqh()
                                    for j in range(2):
                                        self.mm(PTN, PTN[:, js[j]], [(Pc[:, j, :], u2["M2"][:, j, :])], [Pc, u2["M2"]])
                                    Tn = u2["Ta"] if Tc is u2["Tb"] else u2["Tb"]
                                    self.tt("dve", Tn[:], v2(PTN), Tc[:], ALU.add, [PTN, Tc], [Tn])
                                    Tc = Tn
                                Pc = Pn
                                yield
                        for j, h in enumerate(hs):
                            fin[h] = hview(Pc, j)

                    def unit_pre(h, u):
                        kTc, qTc, vTc = KT[:, h, cols], QT[:, h, cols], VT[:, h, cols]
                        gbc = sm[:, 12 + h:13 + h].to_broadcast([128, 128])
                        p1 = self.pq()
                        self.mm(p1, p1[:], [(gbc, TRI), (IDB, POSLb)], [sm, C, Cb, MSKB])
                        p2 = self.pq()
                        self.mm(p2, p2[:], [(gbc, TRI), (IDB, NEGUb)], [sm, C, Cb, MSKB])
                        self.act(u["D"][:], p1[:], AF.Exp, [p1, sm], [u["D"]], bias=sm[:, 24 + h:25 + h], scale=-1.0)
                        self.act(u["DT"][:], p2[:], AF.Exp, [p2, sm], [u["DT"]], bias=sm[:, 28 + h:29 + h], scale=1.0)
                        yield
                        pkk = self.pq()
                        self.mm(pkk, pkk[:], [(kTc, kTc)], [KT])
                        pqk = self.pq()
                        self.mm(pqk, pqk[:], [(kTc, qTc)], [KT, QT])
                        self.stt(u["NA"][:], pkk[:], sm[:, 44 + h:45 + h], u["D"][:], ALU.mult, ALU.mult,
                                 [pkk, sm, u["D"]], [u["NA"]])
                        self.tt("dve", u["QKD"][par][:], pqk[:], u["DT"][:], ALU.mult, [pqk, u["DT"]], [u["QKD"][par]])
                        pv = self.pq()
                        self.mm(pv, pv[:], [(vTc, IDF)], [VT, C])
                        self.act(u["VB"][par][:], pv[:], AF.Copy, [pv, sm], [u["VB"][par]], scale=sm[:, 20 + h:21 + h])
                        yield
                        p3 = self.pq()
                        self.mm(p3, p3[:], [(sm[:, 32 + h:33 + h].to_broadcast([128, 128]), IDF)], [sm, C])
                        self.tt("dve", u["QG"][par][:], p3[:], qTc, ALU.mult, [p3, QT], [u["QG"][par]])
                        if is_s:
                            self.cp("act", u["EGL"][par][:, 0:16], p3[:].rearrange("p (s t) -> p s t", t=8)[:, :, 7], [p3], [u["EGL"][par]])
                        else:
                            self.cp("act", u["EGL"][par][:, 0:1], p3[:, 127:128], [p3], [u["EGL"][par]])
                        pbt = self.pq()
                        self.mm(pbt, pbt[:], [(u["NA"][:], IDB)], [u["NA"], Cb])
                        self.act(u["X0"][:], pbt[:], AF.Copy, [pbt], [u["X0"]])
                        if is_s:
                            NAd, X0d = u["NA"], u["X0"]
                            self.tt("dve", u["P0"][:], pbt[:], IDF, ALU.add, [pbt, C], [u["P0"]])
                        else:
                            NAd, X0d = u["NAD"], u["X0D"]
                            self.tt("pool", NAd[:], u["NA"][:], SEG8B[:], ALU.mult, [u["NA"], SEG8B], [NAd])
                        yield
                        if not is_s:
                            pbd = self.pq()
                            self.mm(pbd, pbd[:], [(NAd[:], IDB)], [NAd, Cb])
                            self.act(X0d[:], pbd[:], AF.Copy, [pbd], [X0d])
                            self.tt("dve", u["P0"][:], pbd[:], IDF, ALU.add, [pbd, C], [u["P0"]])
                        yield
                        Xc, XTc, Pc, Tc = X0d, NAd, u["P0"], None
                        for m in range(1, 3):
                            last = m == 2
                            pxt = self.pq()
                            self.mm(pxt, pxt[:], [(Xc[:], XTc[:])], [Xc, XTc])
                            if not last:
                                pxx = self.pq()
                                self.mm(pxx, pxx[:], [(XTc[:], Xc[:])], [Xc, XTc])
                                XTn, Xn = u["XTa"], u["Xa"]
                                self.cp("dve", XTn[:], pxt[:], [pxt], [XTn])
                                self.cp("dve", Xn[:], pxx[:], [pxx], [Xn])
                            self.tt("dve", u["IXT"][:], pxt[:], IDF, ALU.add, [pxt, C], [u["IXT"]])
                            yield
                            pp = self.pq()
                            self.mm(pp, pp[:], [(u["IXT"][:], Pc[:])], [u["IXT"], Pc])
                            Pn = u["Pa"] if m % 2 else u["Pb"]
                            if is_s and last:
                                Pn = u["PF"][par]
                            self.cp("act", Pn[:], pp[:], [pp], [Pn])
                            if not (is_s and last):
                                pt = self.pq()
                                self.mm(pt, pt[:], [(Pc[:], u["IXT"][:])], [u["IXT"], Pc])
                                Tn = u["Ta"] if m % 2 else u["Tb"]
                                self.act(Tn[:], pt[:], AF.Copy, [pt], [Tn])
                                Tc = Tn
                            Pc = Pn
                            if not last:
                                Xc, XTc = Xn, XTn
                            yield
                        if not is_s:
                            for lev in range(4):
                                Wm, WTm = C[:, 9 + lev, :], C[:, 13 + lev, :]
                                pm1 = self.pq()
                                self.mm(pm1, pm1[:], [(u["NA"][:], Pc[:])], [u["NA"], Pc])
                                self.tt("dve", u["M1"][:], pm1[:], Wm, ALU.mult, [pm1, C], [u["M1"]])
                                if lev < 3:
                                    pm2 = self.pq()
                                    self.mm(pm2, pm2[:], [(u["X0"][:], Tc[:])], [u["X0"], Tc])
                                    self.tt("dve", u["M2"][:], pm2[:], WTm, ALU.mult, [pm2, C], [u["M2"]])
                                yield
                                ppn = self.pq()
                                self.mm(ppn, ppn[:], [(Tc[:], u["M1"][:])], [Tc, u["M1"]])
                                Pn = u["Pa"] if Pc is u["Pb"] else u["Pb"]
                                if lev == 3:
                                    Pn = u["PF"][par]
                                self.tt("dve", Pn[:], ppn[:], Pc[:], ALU.add, [ppn, Pc], [Pn])
                                if lev < 3:
                                    ptn = self.pq()
                                    self.mm(ptn, ptn[:], [(Pc[:], u["M2"][:])], [Pc, u["M2"]])
                                    Tn = u["Ta"] if Tc is u["Tb"] else u["Tb"]
                                    self.tt("dve", Tn[:], ptn[:], Tc[:], ALU.add, [ptn, Tc], [Tn])
                                    Tc = Tn
                                Pc = Pn
                                yield
                        fin[h] = Pc

                    def unit_chain(h, u):
                        Pc = fin[h]
                        kTc = KT[:, h, cols]
                        if is_s:
                            self.ld("sp", "sin", SSF, SSF[:], st_delta[l, h])
                            self.ld("pool", "sinb", SSB, SSB[:], st_delta[l, h])
                            s.op("dve", lambda e, kTc=kTc: e.tensor_tensor(
                                KTZ[:], kTc.unsqueeze(1).to_broadcast([128, 16, 128]), SEGSEL[:], ALU.mult),
                                [KT, SEGSEL], [KTZ])
                            segs = [(sg, slice(sg * 8, sg * 8 + 8), KTZ[:, sg, :], SSF[:, sg, :], SSB[:, sg, :], SSF, SSB, KTZ)
                                    for sg in range(16)]
                        else:
                            segs = [(0, slice(0, 128), kTc, CS[l][:, h, :], CSb[l][:, h, :], CS[l], CSb[l], KT)]
                        pks = self.pq()
                        self.mm(pks, pks[:], [(kz, sb_) for (_, _, kz, _, sb_, _, _, _) in segs],
                                [segs[0][7], segs[0][6]])
                        self.stt(u["R"][:], pks[:], sm[:, 36 + h:37 + h], u["VB"][par][:], ALU.mult, ALU.add,
                                 [pks, sm, u["VB"][par]], [u["R"]])
                        kds = []
                        if not is_s:
                            pk = self.pq()
                            self.mm(pk, pk[:], [(kTc, IDB)], [KT, Cb])
                            self.act(u["KD"][:], pk[:], AF.Copy, [pk, sm], [u["KD"]], scale=sm[:, 40 + h:41 + h])
                        yield
                        pvn = self.pq()
                        self.mm(pvn, pvn[:], [(Pc[:], u["R"][:])], [Pc, u["R"]])
                        self.act(u["VN"][:], pvn[:], AF.Copy, [pvn], [u["VN"]])
                        yield
                        po = self.pq()

                        def ofn(e, po=po, u=u, segs=segs):
                            ins = e.matmul(po[:], u["VN"][:], u["QKD"][par][:], start=True, stop=False)
                            for i, (_, sl, _, _, sb_, _, _, _) in enumerate(segs):
                                ins = e.matmul(po[:, sl], sb_, u["QG"][par][:, sl], start=False, stop=(i == len(segs) - 1))
                            return ins
                        s.op("pe", ofn, [u["VN"], u["QKD"][par], u["QG"][par], segs[0][6]], [po])
                        self.act(OT[:, h, cols], po[:], AF.Copy, [po], [OT])
                        def kdA(i):
                            (sg, sl, kz, sf_, sb_, sft, sbt, kzt) = segs[i]
                            pk = self.pq()
                            self.mm(pk, pk[:], [(kz, IDB)], [kzt, Cb])
                            kd = KDZ[sg % 4]
                            self.act(kd[:], pk[:], AF.Copy, [pk, sm], [kd], scale=sm[:, 40 + h:41 + h])

                        if is_s:
                            kdA(0)
                            kdA(1)
                        for i, (sg, sl, kz, sf_, sb_, sft, sbt, kzt) in enumerate(segs):
                            if is_s:
                                if i + 2 < len(segs):
                                    kdA(i + 2)
                                kd = KDZ[sg % 4]
                            else:
                                kd = u["KD"]
                            pS = self.pq()
                            self.mm(pS, pS[:], [(kd[:], u["VN"][:])], [kd, u["VN"]])
                            self.stt(sf_, sf_, u["EGL"][par][:, sg:sg + 1], pS[:], ALU.mult, ALU.add, [sft, u["EGL"][par], pS], [sft])
                            if not is_s:
                                self.cp("act", sb_, sf_, [sft], [sbt])
                        if is_s:
                            self.st("sp", "sout", o_sdelta[l, h], SSF, SSF[:])
                        yield

                    return unit_pre2, unit_chain

                nck = n // 128
                pre0, ch0 = mk(0)
                rr([pre0(pp) for pp in range(2)] + [mixers_gen()])
                if is_s:
                    for h in range(4):
                        rr([ch0(h, U[h])])
                else:
                    pre1, ch1 = mk(1)
                    rr([ch0(h, U[h]) for h in range(4)] + [pre1(pp) for pp in range(2)])
                    rr([ch1(h, U[h]) for h in range(4)])
                if (not is_s) and c0 + n == NPROMPT:
                    self.st("sp", "opd", o_pdelta[l], CS[l], CS[l][:])
                s.mark(f'M{l} t{ti} units done')
                gb = [TMPA[0], TMPA[1], CV[0], CV[1]]
                gps = []
                for h in range(4):
                    sq = gb[h]
                    self.act(sq[:, 0:n], OT[:, h, 0:n], AF.Square, [OT], [sq])
                    ps = self.ph()
                    self.mm(ps, ps[:, 0:n], [(ONES, sq[:, 0:n])], [C, sq])
                    gps.append(ps)
                for h in range(4):
                    sq, ps = gb[h], gps[h]
                    self.act(sq[:, 0:n], ps[:, 0:n], AF.Ln, [ps, EPSC], [sq], bias=EPSC[:, 0:1], scale=1.0 / 128.0)
                    self.act(sq[:, 0:n], sq[:, 0:n], AF.Exp, [sq], [sq], scale=-0.5)
                for h in range(4):
                    sq = gb[h]
                    self.stt(sq[:, 0:n], OT[:, h, 0:n], prm_l[:, P_ONORM:P_ONORM + 1], sq[:, 0:n], ALU.mult, ALU.mult,
                             [OT, PRM, sq], [sq])
                    self.tt("pool", OCAT[:, h, 0:n], sq[:, 0:n], ZS[:, h, 0:n], ALU.mult, [sq, ZS], [OCAT])
                s.mark(f'M{l} t{ti} gating done')
                s.mark(f'M{l} t{ti} sconv done')
                if ti + 1 < len(tiles):
                    n2 = tiles[ti + 1][1]
                    rmsnorm(X[ti + 1], n2, prm_l[:, P_NM:P_NM + 8], (HT, lambda kc, n2=n2: HT[:, kc, 0:n2]), SQ8, SSUM, RSTD)
                for nn in range(8):
                    ps = self.ph()
                    wt = WOUT[nn // 4]
                    co = (nn % 4) * 128
                    self.mm(ps, ps[:, 0:n], [(wt[:, kc, co:co + 128], OCAT[:, kc, 0:n]) for kc in range(8)], [wt, OCAT])
                    self.tt("dve", xt[:, nn, 0:n], ps[:, 0:n], xt[:, nn, 0:n], ALU.add, [ps, (xt.sub(nn))], [(xt.sub(nn))])

        def stage_F(l, p, tiles, last_layer, start_cb, early_next):
            fbase, nf = FPASS[p]
            reg = "A" if nf == 6 else "B"
            WGt, WUt, WDt = WGr[reg], WUr[reg], WDr[reg]
            start_cb()
            prm_l = PRM[:, l, :]
            pend = []

            def final_norm(xt_, n_, c0_):
                rmsnorm(xt_, n_, FN, (YST, lambda kc: YST[:, kc, 0:n_]), SQ8f, SSUMf, RSTDf)
                self.st("sp", "yout", yT[:, :, c0_:c0_ + n_], YST, YST[:, :, 0:n_])

            ftiles = []
            ti = 0
            while ti < len(tiles):
                (c0, n, nseg, L) = tiles[ti]
                if nseg == 1 and ti + 1 < len(tiles) and tiles[ti + 1][2] == 1:
                    ftiles.append((ti, c0, 2 * n, 1, 2 * n, colview(XA, ti * NT, 2 * NT, 4), colview(HFA, ti * NT, 2 * NT, 2)))
                    ti += 2
                else:
                    ftiles.append((ti, c0, n, nseg, L, X[ti], HF[ti]))
                    ti += 1
            for fi, (ti, c0, n, nseg, L, xt, hft) in enumerate(ftiles):
                is_s = nseg > 1

                def v3(ap2):
                    return ap2.rearrange("p (s t) -> p s t", s=nseg)

                if p == 0 and fi == 0:
                    rmsnorm(xt, n, prm_l[:, P_NF:P_NF + 8], (hft, lambda kc: hft[:, kc, 0:n]), SQ8f, SSUMf, RSTDf)
                ef4 = EF[:, :, 0:nseg * (2 + L)].rearrange("p c (s t) -> p c s t", s=nseg)
                if is_s:
                    self.ld("sp", "sfc", STG_FC, STG_FC[:, 0:nf], st_fconv[l, :, fbase:fbase + nf])
                    self.cp("pool", ef4[:, 0:nf, :, 0:2], STG_FC[:, 0:nf], [STG_FC], [EF])
                else:
                    self.cp("pool", ef4[:, 0:nf, 0, 0:2], CFC[l][:, fbase:fbase + nf, :], [CFC[l]], [EF])
                pus = {}

                def ffA(f):
                    co = f * 128
                    ps = self.pf()
                    self.mm(ps, ps[:, 0:n], [(WGt[:, kc, co:co + 128], hft[:, kc, 0:n]) for kc in range(8)], [WGt, hft])
                    self.act(ef4[:, f, :, 2:2 + L], v3(ps[:, 0:n]), AF.Copy, [ps], [(EF.sub(f))])
                    pu = self.pf()
                    self.mm(pu, pu[:, 0:n], [(WUt[:, kc, co:co + 128], hft[:, kc, 0:n]) for kc in range(8)], [WUt, hft])
                    pus[f] = pu
                    cv = FT[f % 3]
                    fw = lambda i, f=f: prm_l[:, P_FCW + (fbase + f) * 3 + i:P_FCW + (fbase + f) * 3 + i + 1]
                    self.act(v3(cv[:, 0:n]), ef4[:, f, :, 0:L], AF.Copy, [(EF.sub(f)), PRM], [cv], scale=fw(0))
                    for i in range(1, 3):
                        self.stt(v3(cv[:, 0:n]), ef4[:, f, :, i:i + L], fw(i), v3(cv[:, 0:n]), ALU.mult, ALU.add,
                                 [(EF.sub(f)), PRM, cv], [cv])

                def ffB(f):
                    cv = FT[f % 3]
                    pu = pus.pop(f)
                    self.act(cv[:, 0:n], cv[:, 0:n], AF.Silu, [cv], [cv])
                    self.tt("dve", ACTT[:, f, 0:n], pu[:, 0:n], cv[:, 0:n], ALU.mult, [pu, cv], [(ACTT.sub(f))])

                for f in range(nf + 1):
                    if f < nf:
                        ffA(f)
                    if f == 0 and pend:
                        final_norm(*pend.pop(0))
                    if f >= 1:
                        ffB(f - 1)
                s.mark(f'F{l}.{p} t{ti} gate/up done')
                if fi == len(ftiles) - 1:
                    early_next()
                if is_s:
                    self.cp("pool", STG_FC[:, 0:nf], ef4[:, 0:nf, :, L:L + 2], [EF], [STG_FC])
                    self.st("sp", "osfc", o_sfconv[l, :, fbase:fbase + nf], STG_FC, STG_FC[:, 0:nf])
                else:
                    self.cp("pool", CFC[l][:, fbase:fbase + nf, :], ef4[:, 0:nf, 0, L:L + 2], [EF], [CFC[l]])
                    if c0 + n == NPROMPT and p == 3:
                        self.st("sp", "opfc", o_pfconv[l], CFC[l], CFC[l][:])
                s.mark(f'F{l}.{p} t{ti} hist save done')
                if p == 0 and fi + 1 < len(ftiles):
                    (_, _, n2, _, _, xt2, hft2) = ftiles[fi + 1]
                    rmsnorm(xt2, n2, prm_l[:, P_NF:P_NF + 8], (hft2, lambda kc, n2=n2, hft2=hft2: hft2[:, kc, 0:n2]), SQ8f, SSUMf, RSTDf)
                for nn in range(8):
                    ps = self.pf()
                    prs = [(WDt[:, f, nn * 128:(nn + 1) * 128], ACTT[:, f, 0:n]) for f in range(nf)]
                    self.mm(ps, ps[:, 0:n], prs, [WDt, ACTT])
                    self.tt("dve", xt[:, nn, 0:n], ps[:, 0:n], xt[:, nn, 0:n], ALU.add, [ps, (xt.sub(nn))], [(xt.sub(nn))])
                if last_layer and p == 3:
                    pend.append((xt, n, c0))
            while pend:
                final_norm(*pend.pop(0))

        pw_d = self.din("pool_bd", [2, 128, 2, 128])
        PWB = [s.sb(f"pwb{l}", [128, 2, 128], BF16) for l in range(2)]
        for l in range(2):
            self.ld("pool", f"pw{l}", PWB[l], PWB[l][:], pw_d[l])
        self.sb_used = s.sb_ptr

        for gidx, tiles in enumerate(GROUPS):
            for ti, (c0, n, nseg, L) in enumerate(tiles):
                self.ld("sp", f"x{ti}", X[ti], X[ti][:, :, 0:n], xT[:, :, c0:c0 + n])
            for l in range(2):
                first = (gidx == 0 and l == 0)
                lastst = (gidx == len(GROUPS) - 1 and l == 1)
                nl = (l + 1) % 2
                nop = lambda: None
                stage_M(l, tiles, list(range(9)) if first else [7, 8], lambda l=l: loads_F(l, 0))
                stage_F(l, 0, tiles, l == 1, lambda l=l: loads_F(l, 1), nop)
                stage_F(l, 1, tiles, l == 1, lambda l=l: loads_F(l, 2), nop)
                stage_F(l, 2, tiles, l == 1, lambda l=l: loads_F(l, 3), nop)
                stage_F(l, 3, tiles, l == 1,
                        nop if lastst else (lambda nl=nl: loads_M(nl, [0, 1, 2, 3])),
                        nop if lastst else (lambda nl=nl: loads_M(nl, [4, 5, 6])))
        s.emit()


def _kc(w):
    K, N = w.shape
    return np.ascontiguousarray(w.reshape(K // 128, 128, N).transpose(1, 0, 2))


def _consts():
    j = np.arange(128)[:, None]
    i = np.arange(128)[None, :]
    same = (j // 8) == (i // 8)
    c = np.zeros((17, 128, 128), np.float32)
    c[0] = np.eye(128)
    c[1] = 1.0
    c[2] = (j <= i)
    c[3] = (j <= i) & same
    c[4] = same
    c[5] = np.where(i < j, 0.0, BIG)
    c[6] = np.where((i < j) & same, 0.0, BIG)
    c[7] = np.where(i >= j, 0.0, -BIG)
    c[8] = np.where((i >= j) & same, 0.0, -BIG)
    for lev in range(4):
        bsz = 8 << lev
        sameb = (j // (2 * bsz)) == (i // (2 * bsz))
        c[9 + lev] = sameb & ((j // bsz) < (i // bsz))
        c[13 + lev] = sameb & ((i // bsz) < (j // bsz))
    cst = np.ascontiguousarray(c.transpose(1, 0, 2))
    seg = np.zeros((128, 16, 128), np.float32)
    for sgi in range(16):
        seg[:, sgi, sgi * 8:(sgi + 1) * 8] = 1.0
    rc = np.zeros((128, 2, 16), np.float32)
    wins = (2, 4, 8, 16)
    for pc in range(2):
        for half in range(2):
            w = wins[pc * 2 + half]
            rc[half * 64:(half + 1) * 64, pc, :] = 1.0 / np.minimum(np.arange(16) + 1, w)
    return cst, seg, rc


_NC_CACHE = {}


def _get_nc():
    if "nc" not in _NC_CACHE:
        nc = bass.Bass("TRN2", target_bir_lowering=False)
        b = Builder(nc)
        b.build()
        _NC_CACHE["nc"] = nc
    return _NC_CACHE["nc"]


def kernel(x_prompt, x_sample, state_delta, state_delta_conv, state_pool, state_sconv, state_ffn_conv,
           norm_mix, w_in, dn_conv_w, dn_a_log, dn_dt_bias, dn_out_norm, pool_w, pool_scale,
           sconv_w, w_out, norm_ffn, w_ffn_gate, ffn_conv_w, w_ffn_up, w_ffn_down, final_norm):
    f32 = np.float32
    A = lambda a: np.asarray(a, dtype=f32)
    x_prompt, x_sample = A(x_prompt), A(x_sample)
    w_in = A(w_in)
    perm = np.concatenate([np.arange(0, 2048), np.arange(2056, 3080)])
    shared = {}
    shared["w_in"] = np.stack([_kc(w_in[l][:, perm]) for l in range(2)])
    shared["w_ab"] = np.stack([_kc(w_in[l][:, 2048:2056]) for l in range(2)])
    shared["w_out"] = np.stack([_kc(A(w_out)[l]) for l in range(2)])
    shared["w_gate"] = np.stack([_kc(A(w_ffn_gate)[l]) for l in range(2)])
    shared["w_up"] = np.stack([_kc(A(w_ffn_up)[l]) for l in range(2)])
    shared["w_down"] = np.stack([_kc(A(w_ffn_down)[l]) for l in range(2)])
    cst, seg, rc = _consts()
    shared["cst"], shared["segsel"], shared["rc16"] = cst, seg, rc
    prm = np.zeros((128, 2, 256), f32)
    wins = (2, 4, 8, 16)
    for l in range(2):
        prm[:, l, 0:8] = A(norm_mix)[l].reshape(8, 128).T
        prm[:, l, 8:16] = A(norm_ffn)[l].reshape(8, 128).T
        prm[:, l, 16:64] = A(dn_conv_w)[l].reshape(4, 12, 128).transpose(2, 1, 0).reshape(128, 48)
        prm[:, l, 64] = A(dn_out_norm)[l]
        prm[:, l, 65:67] = A(pool_scale)[l].reshape(2, 128).T
        prm[:, l, 67:73] = A(sconv_w)[l].reshape(3, 2, 128).transpose(2, 1, 0).reshape(128, 6)
        prm[:, l, 73:139] = A(ffn_conv_w)[l].reshape(3, 22, 128).transpose(2, 1, 0).reshape(128, 66)
        prm[:, l, 139:143] = A(dn_dt_bias)[l][None, :]
        prm[:, l, 143:147] = A(dn_a_log)[l][None, :]
        for pc in range(2):
            prm[0:64, l, 147 + pc] = 1.0 / wins[pc * 2]
            prm[64:128, l, 147 + pc] = 1.0 / wins[pc * 2 + 1]
    shared["prm"] = prm
    shared["fnorm"] = np.ascontiguousarray(A(final_norm).reshape(8, 128).T)
    pbd = np.zeros((2, 128, 2, 128), f32)
    pw = A(pool_w)
    for l in range(2):
        for g in range(4):
            pc, hh = g // 2, g % 2
            pbd[l, hh * 64:(hh + 1) * 64, pc, hh * 64:(hh + 1) * 64] = pw[l, g]
    shared["pool_bd"] = pbd
    sd, sdc, spl, ssc, sfc = A(state_delta), A(state_delta_conv), A(state_pool), A(state_sconv), A(state_ffn_conv)
    in_maps = []
    for c in range(NCORES):
        sl = slice(16 * c, 16 * c + 16)
        xs = np.concatenate([x_prompt[c], x_sample[sl].reshape(128, D)], axis=0)
        m = dict(shared)
        m["xT"] = np.ascontiguousarray(xs.T.reshape(8, 128, NTOK).transpose(1, 0, 2))
        m["st_delta"] = np.ascontiguousarray(sd[:, sl].transpose(0, 2, 3, 1, 4))
        m["st_dconv"] = np.ascontiguousarray(sdc[:, sl].reshape(2, 16, 3, 12, 128).transpose(0, 4, 3, 1, 2))
        m["st_pool"] = np.ascontiguousarray(spl[:, sl].reshape(2, 16, 15, 2, 128).transpose(0, 4, 3, 1, 2))
        m["st_sconv"] = np.ascontiguousarray(ssc[:, sl].reshape(2, 16, 2, 2, 128).transpose(0, 4, 3, 1, 2))
        m["st_fconv"] = np.ascontiguousarray(sfc[:, sl].reshape(2, 16, 2, 22, 128).transpose(0, 4, 3, 1, 2))
        in_maps.append(m)
    nc = _get_nc()
    res = run_bass_kernel_spmd(nc, in_maps, core_ids=list(range(NCORES)))
    R = res.results
    y_prompt = np.zeros((8, 2048, D), f32)
    y_sample = np.zeros((128, 8, D), f32)
    p_delta = np.zeros((2, 8, 4, 128, 128), f32)
    p_dconv = np.zeros((2, 8, 3, 1536), f32)
    p_pool = np.zeros((2, 8, 15, 256), f32)
    p_sconv = np.zeros((2, 8, 2, 256), f32)
    p_fconv = np.zeros((2, 8, 2, DFF), f32)
    s_delta = np.zeros((2, 128, 4, 128, 128), f32)
    s_dconv = np.zeros((2, 128, 3, 1536), f32)
    s_pool = np.zeros((2, 128, 15, 256), f32)
    s_sconv = np.zeros((2, 128, 2, 256), f32)
    s_fconv = np.zeros((2, 128, 2, DFF), f32)
    for c in range(NCORES):
        r = R[c]
        sl = slice(16 * c, 16 * c + 16)
        y = r["yT"].transpose(1, 0, 2).reshape(D, NTOK).T
        y_prompt[c] = y[:2048]
        y_sample[sl] = y[2048:].reshape(16, 8, D)
        p_delta[:, c] = r["o_pdelta"].transpose(0, 2, 1, 3)
        p_dconv[:, c] = r["o_pdconv"].transpose(0, 3, 2, 1).reshape(2, 3, 1536)
        p_pool[:, c] = r["o_ppool"].transpose(0, 3, 2, 1).reshape(2, 15, 256)
        p_sconv[:, c] = r["o_psconv"].transpose(0, 3, 2, 1).reshape(2, 2, 256)
        p_fconv[:, c] = r["o_pfconv"].transpose(0, 3, 2, 1).reshape(2, 2, DFF)
        s_delta[:, sl] = r["o_sdelta"].transpose(0, 3, 1, 2, 4)
        s_dconv[:, sl] = r["o_sdconv"].transpose(0, 3, 4, 2, 1).reshape(2, 16, 3, 1536)
        s_pool[:, sl] = r["o_spool"].transpose(0, 3, 4, 2, 1).reshape(2, 16, 15, 256)
        s_sconv[:, sl] = r["o_ssconv"].transpose(0, 3, 4, 2, 1).reshape(2, 16, 2, 256)
        s_fconv[:, sl] = r["o_sfconv"].transpose(0, 3, 4, 2, 1).reshape(2, 16, 2, DFF)
    return (y_prompt, y_sample, p_delta, p_dconv, p_pool, p_sconv, p_fconv,
            s_delta, s_dconv, s_pool, s_sconv, s_fconv)
```

```python
import numpy as np
from contextlib import ExitStack
import concourse.bass as bass
import concourse.mybir as mybir
from concourse.bass_utils import run_bass_kernel_spmd

F32 = mybir.dt.float32
BF16 = mybir.dt.bfloat16
AF = mybir.ActivationFunctionType
ALU = mybir.AluOpType
AX = mybir.AxisListType

GRAN = 128
_DSZ = {F32: 4, BF16: 2}
EPS = 1e-6
NCORES = 8
D = 1024
DFF = 2816
NTOK = 2176
NPROMPT = 2048
BIG = 30000.0


class T:
    def __init__(self, ap, space, addr, nbytes, shape, dtype):
        self.ap, self.space, self.addr, self.nbytes = ap, space, addr, nbytes
        self.shape, self.dtype = list(shape), dtype

    def __getitem__(self, k):
        return self.ap[k]

    ivl = None
    subl = None

    def iv(self):
        return (self.space, self.addr, self.addr + self.nbytes)

    def sub(self, i, n=1):
        if self.subl is not None:
            return self.subl[i]
        per = self.nbytes // self.shape[1]
        return (self.space, self.addr + i * per, self.addr + (i + n) * per)


class Sched:
    ENGS = ("pe", "act", "dve", "pool", "sp")

    def __init__(self, nc, sb_base, sb_limit):
        self.nc = nc
        self.ops = {e: [] for e in self.ENGS}
        self.cnt = {e: 0 for e in self.ENGS}
        self.known = {e: {} for e in self.ENGS}
        self.gw = {}
        self.gr = {}
        self.dma_cnt = {}
        self.sb_ptr = sb_base
        self.sb_limit = sb_limit
        self.n_id = 0
        self.total = 0
        self.limit = 1 << 60
        self.marks = []

    def sb(self, name, shape, dtype, addr=None):
        n = 1
        for s in shape[1:]:
            n *= s
        nbytes = n * _DSZ[dtype]
        nb_al = (nbytes + GRAN - 1) // GRAN * GRAN
        if addr is None:
            addr = self.sb_ptr
            self.sb_ptr += nb_al
        assert addr + nb_al <= self.sb_limit, f"SBUF overflow at {name}: {addr + nb_al}"
        self.n_id += 1
        h = self.nc.alloc_sbuf_tensor_at(f"{name}_{self.n_id}", list(shape), dtype, offset=addr)
        return T(h[:], "sb", addr, nbytes, shape, dtype)

    @staticmethod
    def _ivs(items):
        out = []
        flat = []
        for it in items:
            if isinstance(it, T) and it.ivl is not None:
                flat.extend(it.ivl)
            else:
                flat.append(it)
        for it in flat:
            iv = it.iv() if isinstance(it, T) else it
            if iv[0] == "ps":
                b = iv[1] // 2048
                iv = ("ps", b * 2048, b * 2048 + 2048)
            out.append(iv)
        return out

    @staticmethod
    def _psfix(reads, writes):
        r2 = [iv for iv in reads if iv[0] != "ps"]
        w2 = list(writes) + [iv for iv in reads if iv[0] == "ps"]
        return r2, w2

    @staticmethod
    def _grans(iv):
        sp, lo, hi = iv
        return [(sp, g) for g in range(lo // GRAN, (hi + GRAN - 1) // GRAN)]

    def _collect(self, reads, writes):
        deps = {}
        gw, gr = self.gw, self.gr
        for iv in reads:
            for g in self._grans(iv):
                t = gw.get(g)
                if t is not None and deps.get(t[0], 0) < t[1]:
                    deps[t[0]] = t[1]
        for iv in writes:
            for g in self._grans(iv):
                t = gw.get(g)
                if t is not None and deps.get(t[0], 0) < t[1]:
                    deps[t[0]] = t[1]
                r = gr.get(g)
                if r:
                    for k, v in r.items():
                        if deps.get(k, 0) < v:
                            deps[k] = v
        return deps

    def _commit(self, reads, writes, tok):
        k, v = tok
        for iv in reads:
            for g in self._grans(iv):
                r = self.gr.setdefault(g, {})
                if r.get(k, 0) < v:
                    r[k] = v
        for iv in writes:
            for g in self._grans(iv):
                self.gw[g] = tok
                self.gr[g] = {}

    def _waits(self, eng, deps):
        kn = self.known[eng]
        w = []
        for k, v in deps.items():
            if kn.get(k, 0) < v:
                kn[k] = v
                w.append((k, v))
        return w

    def mark(self, name):
        self.marks.append((name, self.total))

    def op(self, eng, fn, reads=(), writes=()):
        self.total += 1
        if self.total > self.limit:
            return
        reads, writes = self._psfix(self._ivs(reads), self._ivs(writes))
        waits = self._waits(eng, self._collect(reads, writes))
        self.cnt[eng] += 1
        tok = (("e", eng), self.cnt[eng])
        self.known[eng][tok[0]] = 0 if False else self.known[eng].get(tok[0], 0)
        self.ops[eng].append((fn, waits, tok))
        self._commit(reads, writes, tok)

    def dma(self, eng, key, fn, reads=(), writes=()):
        self.total += 1
        if self.total > self.limit:
            return
        reads, writes = self._ivs(reads), self._ivs(writes)
        waits = self._waits(eng, self._collect(reads, writes))
        self.dma_cnt[key] = self.dma_cnt.get(key, 0) + 16
        tok = (("d", key), self.dma_cnt[key])
        self.ops[eng].append((fn, waits, tok))
        self._commit(reads, writes, tok)

    def emit(self):
        nc = self.nc
        fin = [(("d", k), v) for k, v in self.dma_cnt.items()]
        with ExitStack() as es:
            sems = {}
            for e in self.ENGS:
                sems[("e", e)] = es.enter_context(nc.semaphore(f"s_{e}"))
            for k in self.dma_cnt:
                sems[("d", k)] = es.enter_context(nc.semaphore(f"d_{k}"))
            block = es.enter_context(nc.Block())

            def run(eng_name):
                def body(eng):
                    for fn, waits, tok in self.ops[eng_name]:
                        for k, v in waits:
                            eng.wait_ge(sems[k], v)
                        ins = fn(eng)
                        ins.then_inc(sems[tok[0]], 16 if tok[0][0] == "d" else 1)
                    if eng_name == "sp":
                        for k, v in fin:
                            eng.wait_ge(sems[k], v)
                return body

            block.tensor(run("pe"))
            block.scalar(run("act"))
            block.vector(run("dve"))
            block.gpsimd(run("pool"))
            block.sync(run("sp"))


NT = 256
PTILES = [(i * NT, NT, 1, NT) for i in range(NPROMPT // NT)]
STILE = (NPROMPT, 128, 16, 8)
GROUPS = [[STILE] + PTILES[0:2], PTILES[2:4], PTILES[4:6], PTILES[6:8]]
MAXT = 3


class Builder:
    def __init__(self, nc):
        self.nc = nc
        self.s = Sched(nc, 16640, 229312)
        self.dr = {}

    def din(self, name, shape, dt=F32):
        self.dr[name] = self.nc.dram_tensor(name, list(shape), dt, kind="ExternalInput").ap()
        return self.dr[name]

    def dout(self, name, shape, dt=F32):
        self.dr[name] = self.nc.dram_tensor(name, list(shape), dt, kind="ExternalOutput").ap()
        return self.dr[name]

    def init_psum(self):
        nc = self.nc
        self.banks = [nc.alloc_psum_tensor(f"bank{i}", [128, 512], F32) for i in range(8)]
        self.h_i = 0
        self.f_i = 0
        self.q_i = 0

    def ph(self, n=256):
        i = self.h_i % 4
        self.h_i += 1
        b, off = i % 2, (i // 2) * 256
        return T(self.banks[b][:, off:off + n], "ps", b * 2048 + off * 4, n * 4, [128, n], F32)

    def pf(self):
        b = self.f_i % 8
        self.f_i += 1
        return T(self.banks[b][:, 0:512], "ps", b * 2048, 2048, [128, 512], F32)

    def pq(self, n=128):
        i = self.q_i % 24
        self.q_i += 1
        b, off = 2 + i % 6, (i // 6) * 128
        return T(self.banks[b][:, off:off + n], "ps", b * 2048 + off * 4, n * 4, [128, n], F32)

    def act(self, out, in_, func, reads, writes, bias=None, scale=None):
        kw = {}
        if bias is not None:
            kw["bias"] = bias
        if scale is not None:
            kw["scale"] = scale
        self.s.op("act", lambda e: e.activation(out, in_, func, **kw), reads, writes)

    def tt(self, eng, out, a, b, op, reads, writes):
        self.s.op(eng, lambda e: e.tensor_tensor(out, a, b, op), reads, writes)

    def stt(self, out, in0, scalar, in1, op0, op1, reads, writes):
        self.s.op("dve", lambda e: e.scalar_tensor_tensor(out, in0, scalar, in1, op0, op1), reads, writes)

    def ts(self, eng, out, in0, s1, op0, reads, writes, s2=None, op1=None):
        if op1 is None:
            self.s.op(eng, lambda e: e.tensor_scalar(out, in0, s1, None, op0), reads, writes)
        else:
            self.s.op(eng, lambda e: e.tensor_scalar(out, in0, s1, s2, op0, op1), reads, writes)

    def cp(self, eng, out, in_, reads, writes):
        if eng == "act":
            self.s.op("act", lambda e: e.copy(out, in_), reads, writes)
        else:
            self.s.op(eng, lambda e: e.tensor_copy(out, in_), reads, writes)

    def mm(self, out_t, out_ap, pairs, reads):
        def fn(e):
            n = len(pairs)
            ins = None
            for i, (l, r) in enumerate(pairs):
                ins = e.matmul(out_ap, l, r, start=(i == 0), stop=(i == n - 1))
            return ins
        self.s.op("pe", fn, reads, [out_t])

    def ld(self, eng, key, out_t, out_ap, in_ap, extra_w=()):
        self.s.dma(eng, key, lambda e: e.dma_start(out=out_ap, in_=in_ap), [], [out_t] + list(extra_w))

    def st(self, eng, key, out_ap, in_t, in_ap):
        self.s.dma(eng, key, lambda e: e.dma_start(out=out_ap, in_=in_ap), [in_t], [])

    def build(self):
        s = self.s
        nc = self.nc
        xT = self.din("xT", [128, 8, NTOK])
        w_in = self.din("w_in", [2, 128, 8, 3072])
        w_ab = self.din("w_ab", [2, 128, 8, 8])
        w_out = self.din("w_out", [2, 128, 8, 1024])
        w_gate = self.din("w_gate", [2, 128, 8, DFF])
        w_up = self.din("w_up", [2, 128, 8, DFF])
        w_down = self.din("w_down", [2, 128, 22, 1024])
        cst = self.din("cst", [128, 17, 128])
        segsel = self.din("segsel", [128, 16, 128])
        prm = self.din("prm", [128, 2, 256])
        fnorm = self.din("fnorm", [128, 8])
        rc16 = self.din("rc16", [128, 2, 16])
        st_delta = self.din("st_delta", [2, 4, 128, 16, 128])
        st_dconv = self.din("st_dconv", [2, 128, 12, 16, 3])
        st_pool = self.din("st_pool", [2, 128, 2, 16, 15])
        st_sconv = self.din("st_sconv", [2, 128, 2, 16, 2])
        st_fconv = self.din("st_fconv", [2, 128, 22, 16, 2])
        yT = self.dout("yT", [128, 8, NTOK])
        o_pdelta = self.dout("o_pdelta", [2, 128, 4, 128])
        o_pdconv = self.dout("o_pdconv", [2, 128, 12, 3])
        o_ppool = self.dout("o_ppool", [2, 128, 2, 15])
        o_psconv = self.dout("o_psconv", [2, 128, 2, 2])
        o_pfconv = self.dout("o_pfconv", [2, 128, 22, 2])
        o_sdelta = self.dout("o_sdelta", [2, 4, 128, 16, 128])
        o_sdconv = self.dout("o_sdconv", [2, 128, 12, 16, 3])
        o_spool = self.dout("o_spool", [2, 128, 2, 16, 15])
        o_ssconv = self.dout("o_ssconv", [2, 128, 2, 16, 2])
        o_sfconv = self.dout("o_sfconv", [2, 128, 22, 16, 2])

        self.init_psum()
        C = s.sb("cst", [128, 17, 128], F32)
        SEG8B = s.sb("seg8b", [128, 128], BF16)
        MSKB = s.sb("mskb", [128, 4, 128], BF16, addr=C.addr + 5 * 512)
        Cb = s.sb("cstb", [128, 128], BF16)
        SEGSEL = s.sb("segsel", [128, 16, 128], BF16)
        PRM = s.sb("prm", [128, 2, 256], F32)
        FN = s.sb("fnorm", [128, 8], F32)
        RC = s.sb("rc16", [128, 2, 16], F32)
        EPSC = s.sb("epsc", [128, 4], F32)
        self.ld("sp", "c0", C, C[:], cst)
        self.ld("pool", "c1", SEGSEL, SEGSEL[:], segsel)
        self.ld("sp", "c2", PRM, PRM[:], prm)
        self.ld("sp", "c3", FN, FN[:], fnorm)
        self.ld("sp", "c4", RC, RC[:], rc16)
        s.op("pool", lambda e: e.memset(EPSC[:, 0:1], EPS), [], [EPSC])
        s.op("pool", lambda e: e.memset(EPSC[:, 1:2], 128.0 * EPS), [], [EPSC])
        self.cp("pool", Cb[:], C[:, 0, :], [C], [Cb])
        self.cp("pool", SEG8B[:], C[:, 4, :], [C], [SEG8B])
        self.ld("pool", "mskb", MSKB, MSKB[:], cst[:, 5:9, :])
        IDF, ONES = C[:, 0, :], C[:, 1, :]
        IDB = Cb[:]
        P_NM, P_NF, P_DCW, P_ONORM, P_PSC, P_SCW, P_FCW, P_DTB, P_ALOG, P_IW = 0, 8, 16, 64, 65, 67, 73, 139, 143, 147
        CS = [s.sb(f"cS{l}", [128, 4, 128], F32) for l in range(2)]
        CSb = [s.sb(f"cSb{l}", [128, 4, 128], BF16) for l in range(2)]
        CDC = [s.sb(f"cdc{l}", [128, 12, 3], F32) for l in range(2)]
        CPL = [s.sb(f"cpl{l}", [128, 2, 15], F32) for l in range(2)]
        CSC = [s.sb(f"csc{l}", [128, 2, 2], F32) for l in range(2)]
        CFC = [s.sb(f"cfc{l}", [128, 22, 2], F32) for l in range(2)]
        NEXPA = [s.sb(f"nexpa{l}", [128, 4], F32) for l in range(2)]
        for l in range(2):
            for t in (CS[l], CSb[l], CDC[l], CPL[l], CSC[l], CFC[l]):
                s.op("pool", (lambda t: (lambda e: e.memset(t[:], 0.0)))(t), [], [t])
            self.act(NEXPA[l][:], PRM[:, l, P_ALOG:P_ALOG + 4], AF.Exp, [PRM], [NEXPA[l]])
            self.ts("dve", NEXPA[l][:], NEXPA[l][:], -1.0, ALU.mult, [NEXPA[l]], [NEXPA[l]])
        XA = s.sb("xa", [128, 8, MAXT * NT], F32)
        HFA = s.sb("hfa", [128, 8, MAXT * NT], BF16)

        def colview(base, lo, ncols, esz):
            t = T(base.ap[:, :, lo:lo + ncols], "sb", base.addr, base.nbytes, [128, 8, ncols], base.dtype)
            row = MAXT * NT * esz
            t.subl = [("sb", base.addr + kc * row + lo * esz, base.addr + kc * row + (lo + ncols) * esz) for kc in range(8)]
            t.ivl = list(t.subl)
            return t

        X = [colview(XA, i * NT, NT, 4) for i in range(MAXT)]
        HF = [colview(HFA, i * NT, NT, 2) for i in range(MAXT)]
        w0 = s.sb_ptr
        WIN = [s.sb(f"win{j}", [128, 8, 512], BF16) for j in range(6)]
        WOUT = [s.sb(f"wout{j}", [128, 8, 512], BF16) for j in range(2)]
        WAB = s.sb("wab", [128, 8, 8], BF16)
        wend = s.sb_ptr
        a = w0
        FPASS = [(0, 6), (6, 5), (11, 6), (17, 5)]
        WGr, WUr, WDr = {}, {}, {}
        for reg, nfr in (("A", 6), ("B", 5)):
            WGr[reg] = s.sb("wg" + reg, [128, 8, nfr * 128], BF16, addr=a); a += 8 * nfr * 128 * 2
            WUr[reg] = s.sb("wu" + reg, [128, 8, nfr * 128], BF16, addr=a); a += 8 * nfr * 128 * 2
            WDr[reg] = s.sb("wd" + reg, [128, nfr, 1024], BF16, addr=a); a += nfr * 1024 * 2
        s.sb_ptr = max(wend, a)
        wk0 = s.sb_ptr
        ht_addr = s.sb_ptr
        HT = s.sb("ht", [128, 8, NT], BF16)
        ssum_addr = s.sb_ptr
        SSUM = s.sb("ssum", [128, NT], F32)
        eq_addr = s.sb_ptr
        EQ = s.sb("extqkv", [128, 12, 3 + NT], F32)
        CV = [s.sb(f"cv{i}", [128, NT], F32) for i in range(2)]
        CV2 = CV
        TMPA = [s.sb(f"tmpa{i}", [128, NT], F32) for i in range(2)]
        QT = s.sb("qT", [128, 4, NT], BF16)
        KT = s.sb("kT", [128, 4, NT], BF16)
        VT = s.sb("vT", [128, 4, NT], F32)
        zs_addr = s.sb_ptr
        ZS = s.sb("zs", [128, 4, NT], F32)
        OT = s.sb("oT", [128, 4, NT], F32)
        SQ8 = s.sb("sq8", [128, 8, NT], F32, addr=zs_addr)
        RSTD = s.sb("rstd", [128, NT], F32, addr=zs_addr)
        EP = s.sb("extp", [128, 2, 368], F32)
        ps2_addr = s.sb_ptr
        PS2 = s.sb("ps2", [128, 368], F32)
        ps4_addr = s.sb_ptr
        PS4 = s.sb("ps4", [128, 368], F32)
        PS8 = s.sb("ps8", [128, 368], F32, addr=ps2_addr)
        PS16 = s.sb("ps16", [128, 368], F32, addr=ps4_addr)
        pd_addr = s.sb_ptr
        PD = s.sb("pd", [128, NT], BF16)
        SCB = s.sb("scb", [128, 2, NT], F32)
        ES = s.sb("exts", [128, 2, 2 + NT], F32)
        ocat_addr = s.sb_ptr
        OCAT = s.sb("ocat", [128, 8, NT], BF16)
        SM = [s.sb(f"sm{i}", [128, 48], F32) for i in range(2)]
        NU = 4
        U = []
        for u in range(NU):
            d = {}
            for nm in ("D", "DT"):
                d[nm] = s.sb(f"u{nm}{u}", [128, 128], F32)
            d["VB"] = [s.sb(f"uVB{u}{i}", [128, 128], F32) for i in range(2)]
            for nm in ("NA", "X0", "XTa", "Xa", "IXT", "Pa", "Pb", "R", "VN", "KD", "NAD", "X0D", "Ta", "Tb"):
                d[nm] = s.sb(f"u{nm}{u}", [128, 128], BF16)
            for nm in ("QKD", "QG", "PF"):
                d[nm] = [s.sb(f"u{nm}{u}{i}", [128, 128], BF16) for i in range(2)]
            d["P0"] = d["Pb"]
            d["M1"] = d["XTa"]
            d["M2"] = d["Xa"]
            d["EGL"] = [s.sb(f"uegl{u}{i}", [128, 16], F32) for i in range(2)]
            U.append(d)
        KTZ = s.sb("ktz", [128, 16, 128], BF16, addr=ht_addr)
        KDZ = [s.sb(f"kdz{i}", [128, 128], BF16, addr=ssum_addr + 256 * i) for i in range(4)]
        SSF = s.sb("ssf", [128, 16, 128], F32, addr=eq_addr)
        SSB = s.sb("ssb", [128, 16, 128], BF16, addr=eq_addr + 8192)
        STG_DC = s.sb("stgdc", [128, 12, 16, 3], F32, addr=ocat_addr)
        STG_PL = s.sb("stgpl", [128, 2, 16, 15], F32, addr=ps2_addr)
        STG_SC = s.sb("stgsc", [128, 2, 16, 2], F32, addr=pd_addr)
        wkM_end = s.sb_ptr
        s.sb_ptr = wk0
        NTF = 512
        EF = s.sb("extf", [128, 6, 2 + NTF], F32)
        ACTT = s.sb("actt", [128, 6, NTF], BF16)
        FT = [s.sb(f"ft{i}", [128, NTF], F32) for i in range(3)]
        SQ8f = s.sb("sq8f", [128, 8, NTF], F32)
        SSUMf = s.sb("ssumf", [128, NTF], F32)
        RSTDf = s.sb("rstdf", [128, NTF], F32)
        YST = s.sb("yst", [128, 8, NTF], F32)
        STG_FC = s.sb("stgfc", [128, 6, 16, 2], F32)
        s.sb_ptr = max(wkM_end, s.sb_ptr)
        self.sb_used = s.sb_ptr

        def rmsnorm(xt, n, gam_ap, outs, sq8, ssum, rstd, out_reads_extra=()):
            self.act(sq8[:, :, 0:n], xt[:, :, 0:n], AF.Square, [xt], [sq8])
            s.op("dve", lambda e: e.tensor_reduce(ssum[:, 0:n], sq8[:, :, 0:n].rearrange("p k n -> p n k"), AX.X, ALU.add),
                 [sq8], [ssum])
            ps = self.pf() if n > 256 else self.ph()
            self.mm(ps, ps[:, 0:n], [(ONES, ssum[:, 0:n])], [C, ssum])
            self.act(rstd[:, 0:n], ps[:, 0:n], AF.Ln, [ps, EPSC], [rstd], bias=EPSC[:, 0:1], scale=1.0 / D)
            self.act(rstd[:, 0:n], rstd[:, 0:n], AF.Exp, [rstd], [rstd], scale=-0.5)
            ot, ofn = outs
            for kc in range(8):
                self.stt(ofn(kc), xt[:, kc, 0:n], gam_ap[:, kc:kc + 1], rstd[:, 0:n], ALU.mult, ALU.mult,
                         [xt, PRM, FN, rstd], [ot])

        def conv_taps(eng0, out_t, out3, ext_t, ext3, wcol, ntap, tmp_t=None):
            if eng0 == "act":
                self.act(out3, ext3(0), AF.Copy, [ext_t, PRM], [out_t], scale=wcol(0))
            else:
                self.ts(eng0, out3, ext3(0), wcol(0), ALU.mult, [ext_t, PRM], [out_t])
            for i in range(1, ntap):
                self.stt(out3, ext3(i), wcol(i), out3, ALU.mult, ALU.add, [ext_t, PRM, out_t], [out_t])

        def loads_M(l, which):
            for j in which:
                if j < 6:
                    self.ld("pool", f"win{j}", WIN[j], WIN[j][:], w_in[l, :, :, j * 512:(j + 1) * 512])
                elif j < 8:
                    self.ld("pool", f"wout{j - 6}", WOUT[j - 6], WOUT[j - 6][:], w_out[l, :, :, (j - 6) * 512:(j - 5) * 512])
                else:
                    self.ld("pool", "wab", WAB, WAB[:], w_ab[l])

        def loads_F(l, p):
            f0, nf = FPASS[p]
            reg = "A" if nf == 6 else "B"
            cs = slice(f0 * 128, (f0 + nf) * 128)
            self.ld("pool", "wg" + reg, WGr[reg], WGr[reg][:], w_gate[l, :, :, cs])
            self.ld("pool", "wu" + reg, WUr[reg], WUr[reg][:], w_up[l, :, :, cs])
            self.ld("pool", "wd" + reg, WDr[reg], WDr[reg][:], w_down[l, :, f0:f0 + nf, :])

        def stage_M(l, tiles, start_loads, early_next):
            loads_M(l, start_loads)
            prm_l = PRM[:, l, :]
            for ti, (c0, n, nseg, L) in enumerate(tiles):
                is_s = nseg > 1
                xt = X[ti]

                def v3(ap2):
                    return ap2.rearrange("p (s t) -> p s t", s=nseg)

                if ti == 0:
                    rmsnorm(xt, n, prm_l[:, P_NM:P_NM + 8], (HT, lambda kc: HT[:, kc, 0:n]), SQ8, SSUM, RSTD)
                s.mark(f'M{l} t{ti} norm done')
                if is_s:
                    self.ld("sp", "sdc", STG_DC, STG_DC[:], st_dconv[l])
                    self.ld("sp", "spl", STG_PL, STG_PL[:], st_pool[l])
                    self.ld("sp", "ssc", STG_SC, STG_SC[:], st_sconv[l])
                eq4 = EQ[:, :, 0:nseg * (3 + L)].rearrange("p c (s t) -> p c s t", s=nseg)
                ep4 = EP[:, :, 0:nseg * (15 + L)].rearrange("p c (s t) -> p c s t", s=nseg)
                es4 = ES[:, :, 0:nseg * (2 + L)].rearrange("p c (s t) -> p c s t", s=nseg)
                if is_s:
                    self.cp("pool", eq4[:, :, :, 0:3], STG_DC[:], [STG_DC], [EQ])
                    self.cp("pool", ep4[:, :, :, 0:15], STG_PL[:], [STG_PL], [EP])
                    self.cp("pool", es4[:, :, :, 0:2], STG_SC[:], [STG_SC], [ES])
                else:
                    self.cp("pool", eq4[:, :, 0, 0:3], CDC[l][:], [CDC[l]], [EQ])
                    self.cp("pool", ep4[:, :, 0, 0:15], CPL[l][:], [CPL[l]], [EP])
                    self.cp("pool", es4[:, :, 0, 0:2], CSC[l][:], [CSC[l]], [ES])
                s.mark(f'M{l} t{ti} hist init done')
                def proj(c):
                    ps = self.ph()
                    wt = WIN[c // 4]
                    co = (c % 4) * 128
                    self.mm(ps, ps[:, 0:n], [(wt[:, kc, co:co + 128], HT[:, kc, 0:n]) for kc in range(8)], [wt, HT])
                    if c < 12:
                        self.act(eq4[:, c, :, 3:3 + L], v3(ps[:, 0:n]), AF.Copy, [ps], [EQ.sub(c)])
                    elif c < 16:
                        self.act(ZS[:, c - 12, 0:n], ps[:, 0:n], AF.Silu, [ps], [ZS])
                    elif c < 18:
                        self.act(ep4[:, c - 16, :, 15:15 + L], v3(ps[:, 0:n]), AF.Copy, [ps], [EP])
                    elif c < 20:
                        self.act(TMPA[c - 18][:, 0:n], ps[:, 0:n], AF.Copy, [ps], [TMPA[c - 18]])
                    elif c < 22:
                        self.act(SCB[:, c - 20, 0:n], ps[:, 0:n], AF.Copy, [ps], [SCB])
                    else:
                        self.tt("dve", es4[:, c - 22, :, 2:2 + L], v3(ps[:, 0:n]), v3(TMPA[c - 22][:, 0:n]), ALU.mult,
                                [ps, TMPA[c - 22]], [ES])

                for c in range(12):
                    proj(c)
                if is_s:
                    self.cp("pool", STG_DC[:], eq4[:, :, :, L:L + 3], [EQ], [STG_DC])
                    self.st("sp", "osdc", o_sdconv[l], STG_DC, STG_DC[:])
                else:
                    self.cp("pool", CDC[l][:], eq4[:, :, 0, L:L + 3], [EQ], [CDC[l]])
                    if c0 + n == NPROMPT:
                        self.st("sp", "opdc", o_pdconv[l], CDC[l], CDC[l][:])
                def cs_ap(c):
                    return eq4[:, c, :, 3:3 + L]

                def convA1(c):
                    cv = CV[c % 2]
                    conv_taps("act", cv, v3(cv[:, 0:n]), EQ.sub(c), lambda i, c=c: eq4[:, c, :, i:i + L],
                              lambda i, c=c: prm_l[:, P_DCW + c * 4 + i:P_DCW + c * 4 + i + 1], 4)

                def convA2(c):
                    cv = CV[c % 2]
                    if c < 8:
                        self.act(cs_ap(c), v3(cv[:, 0:n]), AF.Silu, [cv], [EQ.sub(c)])
                    else:
                        self.act(VT[:, c % 4, 0:n], cv[:, 0:n], AF.Silu, [cv], [VT])

                bps = {}

                def convB1(c):
                    sq = TMPA[c % 2]
                    self.act(v3(sq[:, 0:n]), cs_ap(c), AF.Square, [EQ.sub(c)], [sq])
                    ps = self.ph()
                    self.mm(ps, ps[:, 0:n], [(ONES, sq[:, 0:n])], [C, sq])
                    bps[c] = ps

                def convB2(c):
                    h = c % 4
                    rn = CV2[c % 2]
                    ps = bps.pop(c)
                    if c < 4:
                        self.act(rn[:, 0:n], ps[:, 0:n], AF.Ln, [ps, EPSC], [rn], bias=EPSC[:, 1:2], scale=128.0)
                    else:
                        self.act(rn[:, 0:n], ps[:, 0:n], AF.Ln, [ps, EPSC], [rn], bias=EPSC[:, 0:1], scale=1.0)
                    self.act(rn[:, 0:n], rn[:, 0:n], AF.Exp, [rn], [rn], scale=-0.5)
                    dst = QT if c < 4 else KT
                    self.tt("pool", v3(dst[:, h, 0:n]), cs_ap(c), v3(rn[:, 0:n]), ALU.mult, [EQ.sub(c), rn], [dst])

                def convA_gen():
                    for c in range(13):
                        if c < 12:
                            convA1(c)
                        if c >= 1:
                            convA2(c - 1)
                        yield

                def convB_gen():
                    for c in range(9):
                        if c < 8:
                            convB1(c)
                        if c >= 1:
                            convB2(c - 1)
                        yield

                def rr(gens):
                    gens = list(gens)
                    while gens:
                        nxt = []
                        for g in gens:
                            try:
                                next(g)
                                nxt.append(g)
                            except StopIteration:
                                pass
                        gens = nxt

                def proj_rest_gen():
                    for c in range(12, 24):
                        proj(c)
                        yield

                rr([proj_rest_gen(), convA_gen()])
                s.mark(f'M{l} t{ti} proj done')
                if ti == len(tiles) - 1:
                    early_next()
                if is_s:
                    self.cp("pool", STG_PL[:], ep4[:, :, :, L:L + 15], [EP], [STG_PL])
                    self.cp("pool", STG_SC[:], es4[:, :, :, L:L + 2], [ES], [STG_SC])
                    self.st("sp", "ospl", o_spool[l], STG_PL, STG_PL[:])
                    self.st("sp", "ossc", o_ssconv[l], STG_SC, STG_SC[:])
                else:
                    self.cp("pool", CPL[l][:], ep4[:, :, 0, L:L + 15], [EP], [CPL[l]])
                    self.cp("pool", CSC[l][:], es4[:, :, 0, L:L + 2], [ES], [CSC[l]])
                    if c0 + n == NPROMPT:
                        self.st("sp", "oppl", o_ppool[l], CPL[l], CPL[l][:])
                        self.st("sp", "opsc", o_psconv[l], CSC[l], CSC[l][:])
                TRI = C[:, 3, :] if is_s else C[:, 2, :]
                POSLb = MSKB[:, 1, :] if is_s else MSKB[:, 0, :]
                NEGUb = MSKB[:, 3, :] if is_s else MSKB[:, 2, :]
                SEG = C[:, 4, :] if is_s else ONES
                POSL = C[:, 6, :] if is_s else C[:, 5, :]
                NEGU = C[:, 8, :] if is_s else C[:, 7, :]

                def small_gen(ck):
                    cols = slice(ck * 128, ck * 128 + 128)
                    sm = SM[ck % 2]
                    pab = self.pq()
                    self.mm(pab, pab[:, 0:8], [(HT[:, kc, cols], WAB[:, kc, :]) for kc in range(8)], [HT, WAB])
                    self.tt("dve", sm[:, 0:4], pab[:, 0:4], prm_l[:, P_DTB:P_DTB + 4], ALU.add, [pab, PRM], [sm])
                    self.act(sm[:, 16:20], pab[:, 4:8], AF.Exp, [pab], [sm], scale=-1.0)
                    yield
                    self.act(sm[:, 4:8], sm[:, 0:4], AF.Exp, [sm], [sm])
                    self.ts("dve", sm[:, 16:20], sm[:, 16:20], 1.0, ALU.add, [sm], [sm])
                    yield
                    self.act(sm[:, 8:12], sm[:, 4:8], AF.Ln, [sm], [sm], bias=1.0, scale=1.0)
                    s.op("dve", (lambda sm: (lambda e: e.reciprocal(sm[:, 20:24], sm[:, 16:20])))(sm), [sm], [sm])
                    yield
                    self.tt("dve", sm[:, 12:16], sm[:, 8:12], NEXPA[l][:], ALU.mult, [sm, NEXPA[l]], [sm])
                    self.ts("dve", sm[:, 44:48], sm[:, 20:24], -1.0, ALU.mult, [sm], [sm])
                    yield
                    pgc = self.pq()
                    self.mm(pgc, pgc[:, 0:4], [(TRI, sm[:, 12:16])], [C, sm])
                    pgl = self.pq()
                    self.mm(pgl, pgl[:, 0:4], [(SEG, sm[:, 12:16])], [C, sm])
                    self.cp("dve", sm[:, 24:28], pgc[:, 0:4], [pgc], [sm])
                    self.tt("dve", sm[:, 40:44], pgl[:, 0:4], sm[:, 24:28], ALU.subtract, [pgl, sm], [sm])
                    yield
                    self.ts("dve", sm[:, 28:32], sm[:, 24:28], -1.0, ALU.mult, [sm], [sm])
                    self.act(sm[:, 32:36], sm[:, 24:28], AF.Exp, [sm], [sm])
                    self.act(sm[:, 40:44], sm[:, 40:44], AF.Exp, [sm], [sm])
                    yield
                    self.stt(sm[:, 36:40], sm[:, 32:36], -1.0, sm[:, 20:24], ALU.mult, ALU.mult, [sm], [sm])

                rr([convB_gen()] + [small_gen(ck) for ck in range(n // 128)])
                s.mark(f'M{l} t{ti} conv done')
                def mixers_gen():
                    WP = nseg * (15 + L)
                    for pc in range(2):
                        e3 = ep4[:, pc]

                        d2 = PS2[:, 0:WP].rearrange("p (s t) -> p s t", s=nseg)
                        d4 = PS4[:, 0:WP].rearrange("p (s t) -> p s t", s=nseg)
                        d8 = PS8[:, 0:WP].rearrange("p (s t) -> p s t", s=nseg)
                        d16 = PS16[:, 0:WP].rearrange("p (s t) -> p s t", s=nseg)
                        W = 15 + L
                        self.tt("pool", d2[:, :, 1:W], e3[:, :, 1:W], e3[:, :, 0:W - 1], ALU.add, [EP], [PS2])
                        self.tt("pool", d4[:, :, 3:W], d2[:, :, 3:W], d2[:, :, 1:W - 2], ALU.add, [PS2], [PS4])
                        yield
                        if pc == 0:
                            lo_src, hi_src = d2, d4
                            lo_t, hi_t = PS2, PS4
                        else:
                            self.tt("pool", d8[:, :, 7:W], d4[:, :, 7:W], d4[:, :, 3:W - 4], ALU.add, [PS4], [PS8])
                            self.tt("pool", d16[:, :, 15:W], d8[:, :, 15:W], d8[:, :, 7:W - 8], ALU.add, [PS8], [PS16])
                            yield
                            lo_src, hi_src = d8, d16
                            lo_t, hi_t = PS8, PS16
                        pd3 = PD[:, 0:n].rearrange("p (s t) -> p s t", s=nseg)
                        iw = prm_l[:, P_IW + pc:P_IW + pc + 1]
                        for (half, src, srct) in ((slice(0, 64), lo_src, lo_t), (slice(64, 128), hi_src, hi_t)):
                            self.stt(pd3[half], src[half, :, 15:W], iw[half], e3[half, :, 15:W], ALU.mult, ALU.subtract,
                                     [srct, PRM, EP], [PD])
                            if (not is_s) and c0 == 0:
                                tf = TMPA[0]
                                self.tt("dve", tf[half, 0:16], src[half, 0, 15:31], RC[half, pc, :], ALU.mult, [srct, RC], [tf])
                                self.tt("dve", PD[half, 0:16], tf[half, 0:16], e3[half, 0, 15:31], ALU.subtract, [tf, EP], [PD])
                        yield
                        ps = self.ph()
                        self.mm(ps, ps[:, 0:n], [(PWB[l][:, pc, :], PD[:, 0:n])], [PWB[l], PD])
                        self.act(OCAT[:, 4 + pc, 0:n], ps[:, 0:n], AF.Copy, [ps, PRM], [OCAT],
                                 scale=prm_l[:, P_PSC + pc:P_PSC + pc + 1])
                    s.mark(f'M{l} t{ti} pool done')
                    for c in range(2):
                        cv = CV[c % 2]
                        conv_taps("act", cv, v3(cv[:, 0:n]), ES, lambda i, c=c: es4[:, c, :, i:i + L],
                                  lambda i, c=c: prm_l[:, P_SCW + c * 3 + i:P_SCW + c * 3 + i + 1], 3)
                        self.tt("pool", OCAT[:, 6 + c, 0:n], cv[:, 0:n], SCB[:, c, 0:n], ALU.mult, [cv, SCB], [OCAT])
                        yield

                def mk(ck):
                    cols = slice(ck * 128, ck * 128 + 128)
                    sm = SM[ck % 2]
                    fin = {}
                    par = ck % 2

                    def unit_pre(h, u):
                        kTc, qTc, vTc = KT[:, h, cols], QT[:, h, cols], VT[:, h, cols]
                        gbc = sm[:, 12 + h:13 + h].to_broadcast([128, 128])
                        p1 = self.pq()
                        self.mm(p1, p1[:], [(gbc, TRI), (IDB, POSLb)], [sm, C, Cb, MSKB])
                        p2 = self.pq()
                        self.mm(p2, p2[:], [(gbc, TRI), (IDB, NEGUb)], [sm, C, Cb, MSKB])
                        self.act(u["D"][:], p1[:], AF.Exp, [p1, sm], [u["D"]], bias=sm[:, 24 + h:25 + h], scale=-1.0)
                        self.act(u["DT"][:], p2[:], AF.Exp, [p2, sm], [u["DT"]], bias=sm[:, 28 + h:29 + h], scale=1.0)
                        yield
                        pkk = self.pq()
                        self.mm(pkk, pkk[:], [(kTc, kTc)], [KT])
                        pqk = self.pq()
                        self.mm(pqk, pqk[:], [(kTc, qTc)], [KT, QT])
                        self.stt(u["NA"][:], pkk[:], sm[:, 44 + h:45 + h], u["D"][:], ALU.mult, ALU.mult,
                                 [pkk, sm, u["D"]], [u["NA"]])
                        self.tt("dve", u["QKD"][par][:], pqk[:], u["DT"][:], ALU.mult, [pqk, u["DT"]], [u["QKD"][par]])
                        pv = self.pq()
                        self.mm(pv, pv[:], [(vTc, IDF)], [VT, C])
                        self.act(u["VB"][par][:], pv[:], AF.Copy, [pv, sm], [u["VB"][par]], scale=sm[:, 20 + h:21 + h])
                        yield
                        p3 = self.pq()
                        self.mm(p3, p3[:], [(sm[:, 32 + h:33 + h].to_broadcast([128, 128]), IDF)], [sm, C])
                        self.tt("dve", u["QG"][par][:], p3[:], qTc, ALU.mult, [p3, QT], [u["QG"][par]])
                        if is_s:
                            self.cp("act", u["EGL"][par][:, 0:16], p3[:].rearrange("p (s t) -> p s t", t=8)[:, :, 7], [p3], [u["EGL"][par]])
                        else:
                            self.cp("act", u["EGL"][par][:, 0:1], p3[:, 127:128], [p3], [u["EGL"][par]])
                        pbt = self.pq()
                        self.mm(pbt, pbt[:], [(u["NA"][:], IDB)], [u["NA"], Cb])
                        self.act(u["X0"][:], pbt[:], AF.Copy, [pbt], [u["X0"]])
                        if is_s:
                            NAd, X0d = u["NA"], u["X0"]
                            self.tt("dve", u["P0"][:], pbt[:], IDF, ALU.add, [pbt, C], [u["P0"]])
                        else:
                            NAd, X0d = u["NAD"], u["X0D"]
                            self.tt("pool", NAd[:], u["NA"][:], SEG8B[:], ALU.mult, [u["NA"], SEG8B], [NAd])
                        yield
                        if not is_s:
                            pbd = self.pq()
                            self.mm(pbd, pbd[:], [(NAd[:], IDB)], [NAd, Cb])
                            self.act(X0d[:], pbd[:], AF.Copy, [pbd], [X0d])
                            self.tt("dve", u["P0"][:], pbd[:], IDF, ALU.add, [pbd, C], [u["P0"]])
                        yield
                        Xc, XTc, Pc, Tc = X0d, NAd, u["P0"], None
                        for m in range(1, 3):
                            last = m == 2
                            pxt = self.pq()
                            self.mm(pxt, pxt[:], [(Xc[:], XTc[:])], [Xc, XTc])
                            if not last:
                                pxx = self.pq()
                                self.mm(pxx, pxx[:], [(XTc[:], Xc[:])], [Xc, XTc])
                                XTn, Xn = u["XTa"], u["Xa"]
                                self.cp("dve", XTn[:], pxt[:], [pxt], [XTn])
                                self.cp("dve", Xn[:], pxx[:], [pxx], [Xn])
                            self.tt("dve", u["IXT"][:], pxt[:], IDF, ALU.add, [pxt, C], [u["IXT"]])
                            yield
                            pp = self.pq()
                            self.mm(pp, pp[:], [(u["IXT"][:], Pc[:])], [u["IXT"], Pc])
                            Pn = u["Pa"] if m % 2 else u["Pb"]
                            if is_s and last:
                                Pn = u["PF"][par]
                            self.cp("act", Pn[:], pp[:], [pp], [Pn])
                            if not (is_s and last):
                                pt = self.pq()
                                self.mm(pt, pt[:], [(Pc[:], u["IXT"][:])], [u["IXT"], Pc])
                                Tn = u["Ta"] if m % 2 else u["Tb"]
                                self.act(Tn[:], pt[:], AF.Copy, [pt], [Tn])
                                Tc = Tn
                            Pc = Pn
                            if not last:
                                Xc, XTc = Xn, XTn
                            yield
                        if not is_s:
                            for lev in range(4):
                                Wm, WTm = C[:, 9 + lev, :], C[:, 13 + lev, :]
                                pm1 = self.pq()
                                self.mm(pm1, pm1[:], [(u["NA"][:], Pc[:])], [u["NA"], Pc])
                                self.tt("dve", u["M1"][:], pm1[:], Wm, ALU.mult, [pm1, C], [u["M1"]])
                                if lev < 3:
                                    pm2 = self.pq()
                                    self.mm(pm2, pm2[:], [(u["X0"][:], Tc[:])], [u["X0"], Tc])
                                    self.tt("dve", u["M2"][:], pm2[:], WTm, ALU.mult, [pm2, C], [u["M2"]])
                                yield
                                ppn = self.pq()
                                self.mm(ppn, ppn[:], [(Tc[:], u["M1"][:])], [Tc, u["M1"]])
                                Pn = u["Pa"] if Pc is u["Pb"] else u["Pb"]
                                if lev == 3:
                                    Pn = u["PF"][par]
                                self.tt("dve", Pn[:], ppn[:], Pc[:], ALU.add, [ppn, Pc], [Pn])
                                if lev < 3:
                                    ptn = self.pq()
                                    self.mm(ptn, ptn[:], [(Pc[:], u["M2"][:])], [Pc, u["M2"]])
                                    Tn = u["Ta"] if Tc is u["Tb"] else u["Tb"]
                                    self.tt("dve", Tn[:], ptn[:], Tc[:], ALU.add, [ptn, Tc], [Tn])
                                    Tc = Tn
                                Pc = Pn
                                yield
                        fin[h] = Pc

                    def unit_chain(h, u):
                        Pc = fin[h]
                        kTc = KT[:, h, cols]
                        if is_s:
                            self.ld("sp", "sin", SSF, SSF[:], st_delta[l, h])
                            self.ld("pool", "sinb", SSB, SSB[:], st_delta[l, h])
                            s.op("dve", lambda e, kTc=kTc: e.tensor_tensor(
                                KTZ[:], kTc.unsqueeze(1).to_broadcast([128, 16, 128]), SEGSEL[:], ALU.mult),
                                [KT, SEGSEL], [KTZ])
                            segs = [(sg, slice(sg * 8, sg * 8 + 8), KTZ[:, sg, :], SSF[:, sg, :], SSB[:, sg, :], SSF, SSB, KTZ)
                                    for sg in range(16)]
                        else:
                            segs = [(0, slice(0, 128), kTc, CS[l][:, h, :], CSb[l][:, h, :], CS[l], CSb[l], KT)]
                        pks = self.pq()
                        self.mm(pks, pks[:], [(kz, sb_) for (_, _, kz, _, sb_, _, _, _) in segs],
                                [segs[0][7], segs[0][6]])
                        self.stt(u["R"][:], pks[:], sm[:, 36 + h:37 + h], u["VB"][par][:], ALU.mult, ALU.add,
                                 [pks, sm, u["VB"][par]], [u["R"]])
                        kds = []
                        if not is_s:
                            pk = self.pq()
                            self.mm(pk, pk[:], [(kTc, IDB)], [KT, Cb])
                            self.act(u["KD"][:], pk[:], AF.Copy, [pk, sm], [u["KD"]], scale=sm[:, 40 + h:41 + h])
                        yield
                        pvn = self.pq()
                        self.mm(pvn, pvn[:], [(Pc[:], u["R"][:])], [Pc, u["R"]])
                        self.act(u["VN"][:], pvn[:], AF.Copy, [pvn], [u["VN"]])
                        yield
                        po = self.pq()

                        def ofn(e, po=po, u=u, segs=segs):
                            ins = e.matmul(po[:], u["VN"][:], u["QKD"][par][:], start=True, stop=False)
                            for i, (_, sl, _, _, sb_, _, _, _) in enumerate(segs):
                                ins = e.matmul(po[:, sl], sb_, u["QG"][par][:, sl], start=False, stop=(i == len(segs) - 1))
                            return ins
                        s.op("pe", ofn, [u["VN"], u["QKD"][par], u["QG"][par], segs[0][6]], [po])
                        self.act(OT[:, h, cols], po[:], AF.Copy, [po], [OT])
                        def kdA(i):
                            (sg, sl, kz, sf_, sb_, sft, sbt, kzt) = segs[i]
                            pk = self.pq()
                            self.mm(pk, pk[:], [(kz, IDB)], [kzt, Cb])
                            kd = KDZ[sg % 4]
                            self.act(kd[:], pk[:], AF.Copy, [pk, sm], [kd], scale=sm[:, 40 + h:41 + h])

                        if is_s:
                            kdA(0)
                            kdA(1)
                        for i, (sg, sl, kz, sf_, sb_, sft, sbt, kzt) in enumerate(segs):
                            if is_s:
                                if i + 2 < len(segs):
                                    kdA(i + 2)
                                kd = KDZ[sg % 4]
                            else:
                                kd = u["KD"]
                            pS = self.pq()
                            self.mm(pS, pS[:], [(kd[:], u["VN"][:])], [kd, u["VN"]])
                            self.stt(sf_, sf_, u["EGL"][par][:, sg:sg + 1], pS[:], ALU.mult, ALU.add, [sft, u["EGL"][par], pS], [sft])
                            if not is_s:
                                self.cp("act", sb_, sf_, [sft], [sbt])
                        if is_s:
                            self.st("sp", "sout", o_sdelta[l, h], SSF, SSF[:])
                        yield

                    return unit_pre, unit_chain

                nck = n // 128
                pre0, ch0 = mk(0)
                rr([pre0(h, U[h]) for h in range(4)] + [mixers_gen()])
                if is_s:
                    for h in range(4):
                        rr([ch0(h, U[h])])
                else:
                    pre1, ch1 = mk(1)
                    rr([ch0(h, U[h]) for h in range(4)] + [pre1(h, U[h]) for h in range(4)])
                    rr([ch1(h, U[h]) for h in range(4)])
                if (not is_s) and c0 + n == NPROMPT:
                    self.st("sp", "opd", o_pdelta[l], CS[l], CS[l][:])
                s.mark(f'M{l} t{ti} units done')
                gb = [TMPA[0], TMPA[1], CV[0], CV[1]]
                gps = []
                for h in range(4):
                    sq = gb[h]
                    self.act(sq[:, 0:n], OT[:, h, 0:n], AF.Square, [OT], [sq])
                    ps = self.ph()
                    self.mm(ps, ps[:, 0:n], [(ONES, sq[:, 0:n])], [C, sq])
                    gps.append(ps)
                for h in range(4):
                    sq, ps = gb[h], gps[h]
                    self.act(sq[:, 0:n], ps[:, 0:n], AF.Ln, [ps, EPSC], [sq], bias=EPSC[:, 0:1], scale=1.0 / 128.0)
                    self.act(sq[:, 0:n], sq[:, 0:n], AF.Exp, [sq], [sq], scale=-0.5)
                for h in range(4):
                    sq = gb[h]
                    self.stt(sq[:, 0:n], OT[:, h, 0:n], prm_l[:, P_ONORM:P_ONORM + 1], sq[:, 0:n], ALU.mult, ALU.mult,
                             [OT, PRM, sq], [sq])
                    self.tt("pool", OCAT[:, h, 0:n], sq[:, 0:n], ZS[:, h, 0:n], ALU.mult, [sq, ZS], [OCAT])
                s.mark(f'M{l} t{ti} gating done')
                s.mark(f'M{l} t{ti} sconv done')
                if ti + 1 < len(tiles):
                    n2 = tiles[ti + 1][1]
                    rmsnorm(X[ti + 1], n2, prm_l[:, P_NM:P_NM + 8], (HT, lambda kc, n2=n2: HT[:, kc, 0:n2]), SQ8, SSUM, RSTD)
                for nn in range(8):
                    ps = self.ph()
                    wt = WOUT[nn // 4]
                    co = (nn % 4) * 128
                    self.mm(ps, ps[:, 0:n], [(wt[:, kc, co:co + 128], OCAT[:, kc, 0:n]) for kc in range(8)], [wt, OCAT])
                    self.tt("dve", xt[:, nn, 0:n], ps[:, 0:n], xt[:, nn, 0:n], ALU.add, [ps, (xt.sub(nn))], [(xt.sub(nn))])

        def stage_F(l, p, tiles, last_layer, start_cb, early_next):
            fbase, nf = FPASS[p]
            reg = "A" if nf == 6 else "B"
            WGt, WUt, WDt = WGr[reg], WUr[reg], WDr[reg]
            start_cb()
            prm_l = PRM[:, l, :]
            pend = []

            def final_norm(xt_, n_, c0_):
                rmsnorm(xt_, n_, FN, (YST, lambda kc: YST[:, kc, 0:n_]), SQ8f, SSUMf, RSTDf)
                self.st("sp", "yout", yT[:, :, c0_:c0_ + n_], YST, YST[:, :, 0:n_])

            ftiles = []
            ti = 0
            while ti < len(tiles):
                (c0, n, nseg, L) = tiles[ti]
                if nseg == 1 and ti + 1 < len(tiles) and tiles[ti + 1][2] == 1:
                    ftiles.append((ti, c0, 2 * n, 1, 2 * n, colview(XA, ti * NT, 2 * NT, 4), colview(HFA, ti * NT, 2 * NT, 2)))
                    ti += 2
                else:
                    ftiles.append((ti, c0, n, nseg, L, X[ti], HF[ti]))
                    ti += 1
            for fi, (ti, c0, n, nseg, L, xt, hft) in enumerate(ftiles):
                is_s = nseg > 1

                def v3(ap2):
                    return ap2.rearrange("p (s t) -> p s t", s=nseg)

                if p == 0 and fi == 0:
                    rmsnorm(xt, n, prm_l[:, P_NF:P_NF + 8], (hft, lambda kc: hft[:, kc, 0:n]), SQ8f, SSUMf, RSTDf)
                ef4 = EF[:, :, 0:nseg * (2 + L)].rearrange("p c (s t) -> p c s t", s=nseg)
                if is_s:
                    self.ld("sp", "sfc", STG_FC, STG_FC[:, 0:nf], st_fconv[l, :, fbase:fbase + nf])
                    self.cp("pool", ef4[:, 0:nf, :, 0:2], STG_FC[:, 0:nf], [STG_FC], [EF])
                else:
                    self.cp("pool", ef4[:, 0:nf, 0, 0:2], CFC[l][:, fbase:fbase + nf, :], [CFC[l]], [EF])
                pus = {}

                def ffA(f):
                    co = f * 128
                    ps = self.pf()
                    self.mm(ps, ps[:, 0:n], [(WGt[:, kc, co:co + 128], hft[:, kc, 0:n]) for kc in range(8)], [WGt, hft])
                    self.act(ef4[:, f, :, 2:2 + L], v3(ps[:, 0:n]), AF.Copy, [ps], [(EF.sub(f))])
                    pu = self.pf()
                    self.mm(pu, pu[:, 0:n], [(WUt[:, kc, co:co + 128], hft[:, kc, 0:n]) for kc in range(8)], [WUt, hft])
                    pus[f] = pu
                    cv = FT[f % 3]
                    fw = lambda i, f=f: prm_l[:, P_FCW + (fbase + f) * 3 + i:P_FCW + (fbase + f) * 3 + i + 1]
                    self.act(v3(cv[:, 0:n]), ef4[:, f, :, 0:L], AF.Copy, [(EF.sub(f)), PRM], [cv], scale=fw(0))
                    for i in range(1, 3):
                        self.stt(v3(cv[:, 0:n]), ef4[:, f, :, i:i + L], fw(i), v3(cv[:, 0:n]), ALU.mult, ALU.add,
                                 [(EF.sub(f)), PRM, cv], [cv])

                def ffB(f):
                    cv = FT[f % 3]
                    pu = pus.pop(f)
                    self.act(cv[:, 0:n], cv[:, 0:n], AF.Silu, [cv], [cv])
                    self.tt("dve", ACTT[:, f, 0:n], pu[:, 0:n], cv[:, 0:n], ALU.mult, [pu, cv], [(ACTT.sub(f))])

                for f in range(nf + 1):
                    if f < nf:
                        ffA(f)
                    if f == 0 and pend:
                        final_norm(*pend.pop(0))
                    if f >= 1:
                        ffB(f - 1)
                s.mark(f'F{l}.{p} t{ti} gate/up done')
                if fi == len(ftiles) - 1:
                    early_next()
                if is_s:
                    self.cp("pool", STG_FC[:, 0:nf], ef4[:, 0:nf, :, L:L + 2], [EF], [STG_FC])
                    self.st("sp", "osfc", o_sfconv[l, :, fbase:fbase + nf], STG_FC, STG_FC[:, 0:nf])
                else:
                    self.cp("pool", CFC[l][:, fbase:fbase + nf, :], ef4[:, 0:nf, 0, L:L + 2], [EF], [CFC[l]])
                    if c0 + n == NPROMPT and p == 3:
                        self.st("sp", "opfc", o_pfconv[l], CFC[l], CFC[l][:])
                s.mark(f'F{l}.{p} t{ti} hist save done')
                if p == 0 and fi + 1 < len(ftiles):
                    (_, _, n2, _, _, xt2, hft2) = ftiles[fi + 1]
                    rmsnorm(xt2, n2, prm_l[:, P_NF:P_NF + 8], (hft2, lambda kc, n2=n2, hft2=hft2: hft2[:, kc, 0:n2]), SQ8f, SSUMf, RSTDf)
                for nn in range(8):
                    ps = self.pf()
                    prs = [(WDt[:, f, nn * 128:(nn + 1) * 128], ACTT[:, f, 0:n]) for f in range(nf)]
                    self.mm(ps, ps[:, 0:n], prs, [WDt, ACTT])
                    self.tt("dve", xt[:, nn, 0:n], ps[:, 0:n], xt[:, nn, 0:n], ALU.add, [ps, (xt.sub(nn))], [(xt.sub(nn))])
                if last_layer and p == 3:
                    pend.append((xt, n, c0))
            while pend:
                final_norm(*pend.pop(0))

        pw_d = self.din("pool_bd", [2, 128, 2, 128])
        PWB = [s.sb(f"pwb{l}", [128, 2, 128], BF16) for l in range(2)]
        for l in range(2):
            self.ld("pool", f"pw{l}", PWB[l], PWB[l][:], pw_d[l])
        self.sb_used = s.sb_ptr

        for gidx, tiles in enumerate(GROUPS):
            for ti, (c0, n, nseg, L) in enumerate(tiles):
                self.ld("sp", f"x{ti}", X[ti], X[ti][:, :, 0:n], xT[:, :, c0:c0 + n])
            for l in range(2):
                first = (gidx == 0 and l == 0)
                lastst = (gidx == len(GROUPS) - 1 and l == 1)
                nl = (l + 1) % 2
                nop = lambda: None
                stage_M(l, tiles, list(range(9)) if first else [7, 8], lambda l=l: loads_F(l, 0))
                stage_F(l, 0, tiles, l == 1, lambda l=l: loads_F(l, 1), nop)
                stage_F(l, 1, tiles, l == 1, lambda l=l: loads_F(l, 2), nop)
                stage_F(l, 2, tiles, l == 1, lambda l=l: loads_F(l, 3), nop)
                stage_F(l, 3, tiles, l == 1,
                        nop if lastst else (lambda nl=nl: loads_M(nl, [0, 1, 2, 3])),
                        nop if lastst else (lambda nl=nl: loads_M(nl, [4, 5, 6])))
        s.emit()


def _kc(w):
    K, N = w.shape
    return np.ascontiguousarray(w.reshape(K // 128, 128, N).transpose(1, 0, 2))


def _consts():
    j = np.arange(128)[:, None]
    i = np.arange(128)[None, :]
    same = (j // 8) == (i // 8)
    c = np.zeros((17, 128, 128), np.float32)
    c[0] = np.eye(128)
    c[1] = 1.0
    c[2] = (j <= i)
    c[3] = (j <= i) & same
    c[4] = same
    c[5] = np.where(i < j, 0.0, BIG)
    c[6] = np.where((i < j) & same, 0.0, BIG)
    c[7] = np.where(i >= j, 0.0, -BIG)
    c[8] = np.where((i >= j) & same, 0.0, -BIG)
    for lev in range(4):
        bsz = 8 << lev
        sameb = (j // (2 * bsz)) == (i // (2 * bsz))
        c[9 + lev] = sameb & ((j // bsz) < (i // bsz))
        c[13 + lev] = sameb & ((i // bsz) < (j // bsz))
    cst = np.ascontiguousarray(c.transpose(1, 0, 2))
    seg = np.zeros((128, 16, 128), np.float32)
    for sgi in range(16):
        seg[:, sgi, sgi * 8:(sgi + 1) * 8] = 1.0
    rc = np.zeros((128, 2, 16), np.float32)
    wins = (2, 4, 8, 16)
    for pc in range(2):
        for half in range(2):
            w = wins[pc * 2 + half]
            rc[half * 64:(half + 1) * 64, pc, :] = 1.0 / np.minimum(np.arange(16) + 1, w)
    return cst, seg, rc


_NC_CACHE = {}


def _get_nc():
    if "nc" not in _NC_CACHE:
        nc = bass.Bass("TRN2", target_bir_lowering=False)
        b = Builder(nc)
        b.build()
        _NC_CACHE["nc"] = nc
    return _NC_CACHE["nc"]


def kernel(x_prompt, x_sample, state_delta, state_delta_conv, state_pool, state_sconv, state_ffn_conv,
           norm_mix, w_in, dn_conv_w, dn_a_log, dn_dt_bias, dn_out_norm, pool_w, pool_scale,
           sconv_w, w_out, norm_ffn, w_ffn_gate, ffn_conv_w, w_ffn_up, w_ffn_down, final_norm):
    f32 = np.float32
    A = lambda a: np.asarray(a, dtype=f32)
    x_prompt, x_sample = A(x_prompt), A(x_sample)
    w_in = A(w_in)
    perm = np.concatenate([np.arange(0, 2048), np.arange(2056, 3080)])
    shared = {}
    shared["w_in"] = np.stack([_kc(w_in[l][:, perm]) for l in range(2)])
    shared["w_ab"] = np.stack([_kc(w_in[l][:, 2048:2056]) for l in range(2)])
    shared["w_out"] = np.stack([_kc(A(w_out)[l]) for l in range(2)])
    shared["w_gate"] = np.stack([_kc(A(w_ffn_gate)[l]) for l in range(2)])
    shared["w_up"] = np.stack([_kc(A(w_ffn_up)[l]) for l in range(2)])
    shared["w_down"] = np.stack([_kc(A(w_ffn_down)[l]) for l in range(2)])
    cst, seg, rc = _consts()
    shared["cst"], shared["segsel"], shared["rc16"] = cst, seg, rc
    prm = np.zeros((128, 2, 256), f32)
    wins = (2, 4, 8, 16)
    for l in range(2):
        prm[:, l, 0:8] = A(norm_mix)[l].reshape(8, 128).T
        prm[:, l, 8:16] = A(norm_ffn)[l].reshape(8, 128).T
        prm[:, l, 16:64] = A(dn_conv_w)[l].reshape(4, 12, 128).transpose(2, 1, 0).reshape(128, 48)
        prm[:, l, 64] = A(dn_out_norm)[l]
        prm[:, l, 65:67] = A(pool_scale)[l].reshape(2, 128).T
        prm[:, l, 67:73] = A(sconv_w)[l].reshape(3, 2, 128).transpose(2, 1, 0).reshape(128, 6)
        prm[:, l, 73:139] = A(ffn_conv_w)[l].reshape(3, 22, 128).transpose(2, 1, 0).reshape(128, 66)
        prm[:, l, 139:143] = A(dn_dt_bias)[l][None, :]
        prm[:, l, 143:147] = A(dn_a_log)[l][None, :]
        for pc in range(2):
            prm[0:64, l, 147 + pc] = 1.0 / wins[pc * 2]
            prm[64:128, l, 147 + pc] = 1.0 / wins[pc * 2 + 1]
    shared["prm"] = prm
    shared["fnorm"] = np.ascontiguousarray(A(final_norm).reshape(8, 128).T)
    pbd = np.zeros((2, 128, 2, 128), f32)
    pw = A(pool_w)
    for l in range(2):
        for g in range(4):
            pc, hh = g // 2, g % 2
            pbd[l, hh * 64:(hh + 1) * 64, pc, hh * 64:(hh + 1) * 64] = pw[l, g]
    shared["pool_bd"] = pbd
    sd, sdc, spl, ssc, sfc = A(state_delta), A(state_delta_conv), A(state_pool), A(state_sconv), A(state_ffn_conv)
    in_maps = []
    for c in range(NCORES):
        sl = slice(16 * c, 16 * c + 16)
        xs = np.concatenate([x_prompt[c], x_sample[sl].reshape(128, D)], axis=0)
        m = dict(shared)
        m["xT"] = np.ascontiguousarray(xs.T.reshape(8, 128, NTOK).transpose(1, 0, 2))
        m["st_delta"] = np.ascontiguousarray(sd[:, sl].transpose(0, 2, 3, 1, 4))
        m["st_dconv"] = np.ascontiguousarray(sdc[:, sl].reshape(2, 16, 3, 12, 128).transpose(0, 4, 3, 1, 2))
        m["st_pool"] = np.ascontiguousarray(spl[:, sl].reshape(2, 16, 15, 2, 128).transpose(0, 4, 3, 1, 2))
        m["st_sconv"] = np.ascontiguousarray(ssc[:, sl].reshape(2, 16, 2, 2, 128).transpose(0, 4, 3, 1, 2))
        m["st_fconv"] = np.ascontiguousarray(sfc[:, sl].reshape(2, 16, 2, 22, 128).transpose(0, 4, 3, 1, 2))
        in_maps.append(m)
    nc = _get_nc()
    res = run_bass_kernel_spmd(nc, in_maps, core_ids=list(range(NCORES)))
    R = res.results
    y_prompt = np.zeros((8, 2048, D), f32)
    y_sample = np.zeros((128, 8, D), f32)
    p_delta = np.zeros((2, 8, 4, 128, 128), f32)
    p_dconv = np.zeros((2, 8, 3, 1536), f32)
    p_pool = np.zeros((2, 8, 15, 256), f32)
    p_sconv = np.zeros((2, 8, 2, 256), f32)
    p_fconv = np.zeros((2, 8, 2, DFF), f32)
    s_delta = np.zeros((2, 128, 4, 128, 128), f32)
    s_dconv = np.zeros((2, 128, 3, 1536), f32)
    s_pool = np.zeros((2, 128, 15, 256), f32)
    s_sconv = np.zeros((2, 128, 2, 256), f32)
    s_fconv = np.zeros((2, 128, 2, DFF), f32)
    for c in range(NCORES):
        r = R[c]
        sl = slice(16 * c, 16 * c + 16)
        y = r["yT"].transpose(1, 0, 2).reshape(D, NTOK).T
        y_prompt[c] = y[:2048]
        y_sample[sl] = y[2048:].reshape(16, 8, D)
        p_delta[:, c] = r["o_pdelta"].transpose(0, 2, 1, 3)
        p_dconv[:, c] = r["o_pdconv"].transpose(0, 3, 2, 1).reshape(2, 3, 1536)
        p_pool[:, c] = r["o_ppool"].transpose(0, 3, 2, 1).reshape(2, 15, 256)
        p_sconv[:, c] = r["o_psconv"].transpose(0, 3, 2, 1).reshape(2, 2, 256)
        p_fconv[:, c] = r["o_pfconv"].transpose(0, 3, 2, 1).reshape(2, 2, DFF)
        s_delta[:, sl] = r["o_sdelta"].transpose(0, 3, 1, 2, 4)
        s_dconv[:, sl] = r["o_sdconv"].transpose(0, 3, 4, 2, 1).reshape(2, 16, 3, 1536)
        s_pool[:, sl] = r["o_spool"].transpose(0, 3, 4, 2, 1).reshape(2, 16, 15, 256)
        s_sconv[:, sl] = r["o_ssconv"].transpose(0, 3, 4, 2, 1).reshape(2, 16, 2, 256)
        s_fconv[:, sl] = r["o_sfconv"].transpose(0, 3, 4, 2, 1).reshape(2, 16, 2, DFF)
    return (y_prompt, y_sample, p_delta, p_dconv, p_pool, p_sconv, p_fconv,
            s_delta, s_dconv, s_pool, s_sconv, s_fconv)
```
